# Optimizing a Trainium2 kernel written in Bass

```python
import math
import jax
import jax.numpy as jnp
from jax import lax
import numpy as np

D_MODEL = 1024
BATCH = 16
SEQ = 2048
DEPTH = 2

MIX_DIM = D_MODEL
HEAD_DIM = 64
RWKV_DIM = MIX_DIM // 2
RWKV_HEADS = RWKV_DIM // HEAD_DIM
RWKV_DECAY_LORA = 64
RWKV_A_LORA = 64
RWKV_GATE_LORA = 128
RWKV_GN_EPS = 64e-5
RWKV_COLS = 3 * RWKV_DIM + RWKV_DECAY_LORA + RWKV_A_LORA + RWKV_GATE_LORA
SSM_DIM = MIX_DIM // 2
SSM_HEAD_DIM = 64
SSM_HEADS = SSM_DIM // SSM_HEAD_DIM
SSM_GROUPS = 2
SSM_STATE = 128
SSM_CONV = 4
SSM_CHUNK = 128
SSM_XBC = SSM_DIM + 2 * SSM_GROUPS * SSM_STATE
SSM_COLS = SSM_DIM + SSM_XBC + SSM_HEADS
L0_COLS = RWKV_COLS + SSM_COLS
SB_DIM = MIX_DIM // 2
SB_HEADS = SB_DIM // HEAD_DIM
MLA_NOPE = 64
MLA_ROPE = 32
MLA_V = 64
MLA_HEADS = (MIX_DIM // 2) // MLA_V
MLA_Q_LORA = 256
MLA_KV_LORA = 128
ROPE_THETA = 10000.0
L1_COLS = 3 * SB_DIM + MLA_Q_LORA + MLA_KV_LORA + MLA_ROPE
Q_BLOCK = 128
D_FF = 2816
FFN_CONV = 3
ALPHA = (2 * DEPTH) ** 0.25
BETA = (8 * DEPTH) ** -0.25

kernel_name = 'hybrid_rwkv7_ssd_stickbreak_mla_convffn'


def _layer_norm(x, g, b, eps=1e-5):
    xf = x.astype(jnp.float32)
    mu = jnp.mean(xf, axis=-1, keepdims=True)
    var = jnp.mean(jnp.square(xf - mu), axis=-1, keepdims=True)
    return ((xf - mu) * lax.rsqrt(var + eps) * g + b).astype(x.dtype)


def _rms_norm(x, g, eps=1e-6):
    xf = x.astype(jnp.float32)
    return (xf * lax.rsqrt(jnp.mean(xf * xf, axis=-1, keepdims=True) + eps) * g).astype(x.dtype)


def _causal_dwconv(u, w, b):
    K = w.shape[0]
    T = u.shape[1]
    up = jnp.pad(u, ((0, 0), (K - 1, 0), (0, 0)))
    y = b + up[:, 0:T] * w[0]
    for i in range(1, K):
        y = y + up[:, i:i + T] * w[i]
    return y


def _to_heads(t, n):
    return t.reshape(t.shape[0], t.shape[1], n, -1)


def _rwkv7_scan(r, w, k, v, a, b):
    bsz, T, H, N = r.shape

    def step(S, inp):
        r_t, w_t, k_t, v_t, a_t, b_t = inp
        sa = jnp.einsum('bhij,bhj->bhi', S, a_t)
        S = S * w_t[:, :, None, :] + sa[..., None] * b_t[:, :, None, :] + v_t[..., None] * k_t[:, :, None, :]
        return S, jnp.einsum('bhij,bhj->bhi', S, r_t)

    xs = tuple(jnp.swapaxes(t, 0, 1) for t in (r, w, k, v, a, b))
    _, y = lax.scan(step, jnp.zeros((bsz, H, N, N), jnp.float32), xs)
    return jnp.swapaxes(y, 0, 1)


def _rwkv7_group(p, mix, w0, w2, a0, a2, g2, k_k, k_a, r_k, ln_g, ln_b):
    bsz, T, _ = p.shape
    prev = jnp.pad(p, ((0, 0), (1, 0), (0, 0)))[:, :-1]
    p = p + (prev - p) * mix
    cuts = [RWKV_DIM, 2 * RWKV_DIM, 3 * RWKV_DIM, 3 * RWKV_DIM + RWKV_DECAY_LORA,
            3 * RWKV_DIM + RWKV_DECAY_LORA + RWKV_A_LORA]
    r, k, v, w_lo, a_lo, g_lo = jnp.split(p, cuts, axis=-1)
    log_w = -jax.nn.softplus(-(w0 + jnp.tanh(w_lo) @ w2)) - 0.5
    decay = jnp.exp(-jnp.exp(log_w.astype(jnp.float32)))
    a = jax.nn.sigmoid(a0 + a_lo @ a2)
    g = jax.nn.sigmoid(g_lo) @ g2
    kk = _to_heads(k * k_k, RWKV_HEADS).astype(jnp.float32)
    kk = kk / jnp.maximum(jnp.sqrt(jnp.sum(kk * kk, axis=-1, keepdims=True)), 1e-12)
    k = k * (1 + (a - 1) * k_a)
    r_h, k_h, v_h, a_h, w_h = [_to_heads(t, RWKV_HEADS).astype(jnp.float32) for t in (r, k, v, a, decay)]
    y = _rwkv7_scan(r_h, w_h, k_h, v_h, -kk, kk * a_h)
    mu = jnp.mean(y, axis=-1, keepdims=True)
    var = jnp.mean(jnp.square(y - mu), axis=-1, keepdims=True)
    y = ((y - mu) * lax.rsqrt(var + RWKV_GN_EPS)).reshape(bsz, T, RWKV_DIM) * ln_g + ln_b
    bonus = jnp.sum(r_h * k_h * r_k, axis=-1, keepdims=True) * v_h
    return ((y + bonus.reshape(bsz, T, RWKV_DIM)) * g).astype(p.dtype)


def _segsum(a):
    L = a.shape[-1]
    ar = jnp.broadcast_to(a[..., :, None], a.shape + (L,))
    strict = jnp.tril(jnp.ones((L, L), bool), -1)
    s = jnp.cumsum(jnp.where(strict, ar, 0.0), axis=-2)
    return jnp.where(jnp.tril(jnp.ones((L, L), bool)), s, -jnp.inf)


def _ssd_chunked(xs, dt, A, Bm, Cm):
    bsz, T, H, P = xs.shape
    G, N = Bm.shape[2], Bm.shape[3]
    J = H // G
    c, l = T // SSM_CHUNK, SSM_CHUNK
    X = (xs * dt[..., None]).reshape(bsz, c, l, G, J, P)
    a_dt = (dt * A).reshape(bsz, c, l, G, J).transpose(0, 3, 4, 1, 2)
    Bc = Bm.reshape(bsz, c, l, G, N)
    Cc = Cm.reshape(bsz, c, l, G, N)
    a_cum = jnp.cumsum(a_dt, axis=-1)
    decay_in = jnp.exp(_segsum(a_dt))
    cb = jnp.einsum('bclgn,bcsgn->bgcls', Cc, Bc)
    y_diag = jnp.einsum('bgjcls,bcsgjp->bclgjp', cb[:, :, None] * decay_in, X)
    decay_to_end = jnp.exp(a_cum[..., -1:] - a_cum)
    states = jnp.einsum('bclgn,bgjcl,bclgjp->bcgjpn', Bc, decay_to_end, X)
    states = jnp.concatenate([jnp.zeros_like(states[:, :1]), states], axis=1)
    chunk_decay = jnp.exp(_segsum(jnp.pad(a_cum[..., -1], ((0, 0), (0, 0), (0, 0), (1, 0)))))
    states = jnp.einsum('bgjzc,bcgjpn->bzgjpn', chunk_decay, states)[:, :-1]
    y_off = jnp.einsum('bclgn,bcgjpn,bgjcl->bclgjp', Cc, states, jnp.exp(a_cum))
    return (y_diag + y_off).reshape(bsz, T, H, P)


def _mamba2_group(p, conv_w, conv_b, dt_bias, a_log, d_skip, norm_g):
    bsz, T, _ = p.shape
    z, xbc, dt_raw = jnp.split(p, [SSM_DIM, SSM_DIM + SSM_XBC], axis=-1)
    xbc = jax.nn.silu(_causal_dwconv(xbc, conv_w, conv_b))
    xs, Bm, Cm = jnp.split(xbc, [SSM_DIM, SSM_DIM + SSM_GROUPS * SSM_STATE], axis=-1)
    xs = xs.reshape(bsz, T, SSM_HEADS, SSM_HEAD_DIM).astype(jnp.float32)
    Bm = Bm.reshape(bsz, T, SSM_GROUPS, SSM_STATE).astype(jnp.float32)
    Cm = Cm.reshape(bsz, T, SSM_GROUPS, SSM_STATE).astype(jnp.float32)
    dt = jax.nn.softplus((dt_raw + dt_bias).astype(jnp.float32))
    A = -jnp.exp(a_log.astype(jnp.float32))
    y = _ssd_chunked(xs, dt, A, Bm, Cm) + xs * d_skip[:, None]
    u = (y.reshape(bsz, T, SSM_DIM) * jax.nn.silu(z.astype(jnp.float32))).reshape(bsz, T, SSM_GROUPS, -1)
    u = u * lax.rsqrt(jnp.mean(u * u, axis=-1, keepdims=True) + 1e-5)
    return (u.reshape(bsz, T, SSM_DIM) * norm_g).astype(p.dtype)


def _mixer_rwkv_ssd(h, w_in, mix, w0, w2, a0, a2, g2, k_k, k_a, r_k, ln_g, ln_b,
                    conv_w, conv_b, dt_bias, a_log, d_skip, norm_g, w_out):
    proj = h @ w_in
    y_a = _rwkv7_group(proj[..., :RWKV_COLS], mix, w0, w2, a0, a2, g2, k_k, k_a, r_k, ln_g, ln_b)
    y_b = _mamba2_group(proj[..., RWKV_COLS:], conv_w, conv_b, dt_bias, a_log, d_skip, norm_g)
    return jnp.concatenate([y_a, y_b], axis=-1) @ w_out


def _stick_breaking(q, k, v):
    T = q.shape[2]
    scale = q.shape[-1] ** -0.5
    outs = []
    for start in range(0, T, Q_BLOCK):
        end = start + Q_BLOCK
        z = jnp.einsum('bhqd,bhkd->bhqk', q[:, :, start:end], k[:, :, :end]).astype(jnp.float32) * scale
        strict = jnp.arange(end)[None, :] < jnp.arange(start, end)[:, None]
        log_keep = jnp.where(strict, jax.nn.log_sigmoid(-z), 0.0)
        log_att = jax.nn.log_sigmoid(z) + lax.cumsum(log_keep, axis=3, reverse=True) - log_keep
        att = jnp.where(strict, jnp.exp(log_att), 0.0)
        outs.append(jnp.einsum('bhqk,bhkd->bhqd', att.astype(v.dtype), v[:, :, :end]))
    return jnp.concatenate(outs, axis=2)


def _rope_tables(positions):
    inv_freq = 1.0 / (ROPE_THETA ** (jnp.arange(0, MLA_ROPE, 2, dtype=jnp.float32) / MLA_ROPE))
    ang = positions.astype(jnp.float32)[..., None] * inv_freq
    return jnp.cos(ang), jnp.sin(ang)


def _apply_rope(x, cos, sin):
    half = x.shape[-1] // 2
    x1, x2 = x[..., :half], x[..., half:]
    return jnp.concatenate([x1 * cos - x2 * sin, x2 * cos + x1 * sin], axis=-1)


def _mla_attention(q_nope, q_pe, k_nope, k_pe, v):
    T = q_nope.shape[2]
    scale = (MLA_NOPE + MLA_ROPE) ** -0.5
    outs = []
    for start in range(0, T, Q_BLOCK):
        end = start + Q_BLOCK
        s = (jnp.einsum('bhqd,bhkd->bhqk', q_nope[:, :, start:end], k_nope[:, :, :end])
             + jnp.einsum('bhqd,bkd->bhqk', q_pe[:, :, start:end], k_pe[:, :end])).astype(jnp.float32) * scale
        causal = jnp.arange(end)[None, :] <= jnp.arange(start, end)[:, None]
        prob = jax.nn.softmax(jnp.where(causal, s, -jnp.inf), axis=-1)
        outs.append(jnp.einsum('bhqk,bhkd->bhqd', prob.astype(v.dtype), v[:, :, :end]))
    return jnp.concatenate(outs, axis=2)


def _mixer_sb_mla(h, positions, w_in, q_norm_g, w_uq, kv_norm_g, w_ukv, w_out):
    bsz, T, _ = h.shape
    proj = h @ w_in
    cuts = [SB_DIM, 2 * SB_DIM, 3 * SB_DIM, 3 * SB_DIM + MLA_Q_LORA, 3 * SB_DIM + MLA_Q_LORA + MLA_KV_LORA]
    q_sb, k_sb, v_sb, c_q, c_kv, k_pe = jnp.split(proj, cuts, axis=-1)
    tr = lambda t: t.transpose(0, 2, 1, 3)
    y_c = _stick_breaking(tr(_to_heads(q_sb, SB_HEADS)), tr(_to_heads(k_sb, SB_HEADS)), tr(_to_heads(v_sb, SB_HEADS)))
    y_c = tr(y_c).reshape(bsz, T, SB_DIM)
    q = _to_heads(_rms_norm(c_q, q_norm_g) @ w_uq, MLA_HEADS)
    kv = _to_heads(_rms_norm(c_kv, kv_norm_g) @ w_ukv, MLA_HEADS)
    cos, sin = _rope_tables(positions)
    q_pe = _apply_rope(q[..., MLA_NOPE:], cos[:, :, None], sin[:, :, None])
    k_pe = _apply_rope(k_pe, cos, sin)
    y_d = _mla_attention(tr(q[..., :MLA_NOPE]), tr(q_pe), tr(kv[..., :MLA_NOPE]), k_pe, tr(kv[..., MLA_NOPE:]))
    y_d = tr(y_d).reshape(bsz, T, MLA_HEADS * MLA_V)
    return jnp.concatenate([y_c, y_d.astype(y_c.dtype)], axis=-1) @ w_out


def _conv_ffn(h, w_up, conv_w, conv_b, w_down):
    gate, up = jnp.split(h @ w_up, [D_FF], axis=-1)
    gate = _causal_dwconv(gate, conv_w, conv_b)
    return (jax.nn.silu(gate) * up) @ w_down


def setup_inputs(seed: int = 0) -> dict:
    key = jax.random.key(seed)
    ks = iter(jax.random.split(key, 64))

    def nrm(shape, scale):
        return jax.random.normal(next(ks), shape, jnp.float32) * scale

    def uni(shape, lo, hi):
        return jax.random.uniform(next(ks), shape, jnp.float32, lo, hi)

    def gain(n):
        return 1.0 + nrm((n,), 0.02)

    inp = {}
    inp['x'] = nrm((BATCH, SEQ, D_MODEL), 1.0)
    inp['positions'] = (jax.random.randint(next(ks), (BATCH, 1), 0, 4096, dtype=jnp.int32)
                        + jnp.arange(SEQ, dtype=jnp.int32)[None, :])
    inp['l0_w_in'] = nrm((D_MODEL, L0_COLS), D_MODEL ** -0.5)
    inp['rwkv_mix'] = uni((RWKV_COLS,), 0.0, 1.0)
    inp['rwkv_w0'] = uni((RWKV_DIM,), -6.0, -1.0)
    inp['rwkv_w2'] = nrm((RWKV_DECAY_LORA, RWKV_DIM), 0.1)
    inp['rwkv_a0'] = nrm((RWKV_DIM,), 0.1)
    inp['rwkv_a2'] = nrm((RWKV_A_LORA, RWKV_DIM), 0.1)
    inp['rwkv_g2'] = nrm((RWKV_GATE_LORA, RWKV_DIM), RWKV_GATE_LORA ** -0.5)
    inp['rwkv_k_k'] = 0.85 + nrm((RWKV_DIM,), 0.05)
    inp['rwkv_k_a'] = 1.0 + nrm((RWKV_DIM,), 0.05)
    inp['rwkv_r_k'] = nrm((RWKV_HEADS, HEAD_DIM), 0.1)
    inp['rwkv_ln_g'] = gain(RWKV_DIM)
    inp['rwkv_ln_b'] = nrm((RWKV_DIM,), 0.02)
    inp['ssm_conv_w'] = nrm((SSM_CONV, SSM_XBC), 0.5)
    inp['ssm_conv_b'] = nrm((SSM_XBC,), 0.02)
    dt0 = jnp.exp(uni((SSM_HEADS,), math.log(1e-3), math.log(1e-1)))
    inp['ssm_dt_bias'] = dt0 + jnp.log(-jnp.expm1(-dt0))
    inp['ssm_a_log'] = jnp.log(uni((SSM_HEADS,), 1.0, 16.0))
    inp['ssm_d'] = 1.0 + nrm((SSM_HEADS,), 0.1)
    inp['ssm_norm_g'] = gain(SSM_DIM)
    inp['l0_w_out'] = nrm((MIX_DIM, D_MODEL), MIX_DIM ** -0.5 * BETA)
    inp['l0_ln1_g'] = gain(D_MODEL)
    inp['l0_ln1_b'] = nrm((D_MODEL,), 0.02)
    inp['ffn0_w_up'] = nrm((D_MODEL, 2 * D_FF), D_MODEL ** -0.5)
    inp['ffn0_conv_w'] = nrm((FFN_CONV, D_FF), FFN_CONV ** -0.5)
    inp['ffn0_conv_b'] = nrm((D_FF,), 0.02)
    inp['ffn0_w_down'] = nrm((D_FF, D_MODEL), D_FF ** -0.5 * BETA)
    inp['l0_ln2_g'] = gain(D_MODEL)
    inp['l0_ln2_b'] = nrm((D_MODEL,), 0.02)
    inp['l1_w_in'] = nrm((D_MODEL, L1_COLS), D_MODEL ** -0.5)
    inp['mla_q_norm_g'] = gain(MLA_Q_LORA)
    inp['mla_w_uq'] = nrm((MLA_Q_LORA, MLA_HEADS * (MLA_NOPE + MLA_ROPE)), MLA_Q_LORA ** -0.5)
    inp['mla_kv_norm_g'] = gain(MLA_KV_LORA)
    inp['mla_w_ukv'] = nrm((MLA_KV_LORA, MLA_HEADS * (MLA_NOPE + MLA_V)), MLA_KV_LORA ** -0.5)
    inp['l1_w_out'] = nrm((MIX_DIM, D_MODEL), MIX_DIM ** -0.5 * BETA)
    inp['l1_ln1_g'] = gain(D_MODEL)
    inp['l1_ln1_b'] = nrm((D_MODEL,), 0.02)
    inp['ffn1_w_up'] = nrm((D_MODEL, 2 * D_FF), D_MODEL ** -0.5)
    inp['ffn1_conv_w'] = nrm((FFN_CONV, D_FF), FFN_CONV ** -0.5)
    inp['ffn1_conv_b'] = nrm((D_FF,), 0.02)
    inp['ffn1_w_down'] = nrm((D_FF, D_MODEL), D_FF ** -0.5 * BETA)
    inp['l1_ln2_g'] = gain(D_MODEL)
    inp['l1_ln2_b'] = nrm((D_MODEL,), 0.02)
    return inp


def reference(x, positions, l0_w_in, rwkv_mix, rwkv_w0, rwkv_w2, rwkv_a0, rwkv_a2, rwkv_g2,
              rwkv_k_k, rwkv_k_a, rwkv_r_k, rwkv_ln_g, rwkv_ln_b, ssm_conv_w, ssm_conv_b,
              ssm_dt_bias, ssm_a_log, ssm_d, ssm_norm_g, l0_w_out, l0_ln1_g, l0_ln1_b,
              ffn0_w_up, ffn0_conv_w, ffn0_conv_b, ffn0_w_down, l0_ln2_g, l0_ln2_b,
              l1_w_in, mla_q_norm_g, mla_w_uq, mla_kv_norm_g, mla_w_ukv, l1_w_out,
              l1_ln1_g, l1_ln1_b, ffn1_w_up, ffn1_conv_w, ffn1_conv_b, ffn1_w_down,
              l1_ln2_g, l1_ln2_b):
    mixers = (_mixer_rwkv_ssd, _mixer_sb_mla)
    mixer_args = (
        (l0_w_in, rwkv_mix, rwkv_w0, rwkv_w2, rwkv_a0, rwkv_a2, rwkv_g2, rwkv_k_k, rwkv_k_a,
         rwkv_r_k, rwkv_ln_g, rwkv_ln_b, ssm_conv_w, ssm_conv_b, ssm_dt_bias, ssm_a_log,
         ssm_d, ssm_norm_g, l0_w_out),
        (positions, l1_w_in, mla_q_norm_g, mla_w_uq, mla_kv_norm_g, mla_w_ukv, l1_w_out),
    )
    ffn_args = ((ffn0_w_up, ffn0_conv_w, ffn0_conv_b, ffn0_w_down),
                (ffn1_w_up, ffn1_conv_w, ffn1_conv_b, ffn1_w_down))
    ln_mix = ((l0_ln1_g, l0_ln1_b), (l1_ln1_g, l1_ln1_b))
    ln_ffn = ((l0_ln2_g, l0_ln2_b), (l1_ln2_g, l1_ln2_b))
    h = x
    for layer in range(DEPTH):
        mixed = mixers[layer % 2](h, *mixer_args[layer])
        h = _layer_norm(ALPHA * h + mixed, *ln_mix[layer])
        h = _layer_norm(ALPHA * h + _conv_ffn(h, *ffn_args[layer]), *ln_ffn[layer])
    return h.astype(x.dtype)
```

```python
import contextlib
import math
import numpy as np
import concourse.bass as bass
import concourse.mybir as mybir
from concourse.bass_utils import run_bass_kernel_spmd

F32 = mybir.dt.float32
BF16 = mybir.dt.bfloat16
I32 = mybir.dt.int32
AF = mybir.ActivationFunctionType
ALU = mybir.AluOpType
AX = mybir.AxisListType

T = 2048
NB = 2
NT = NB * T
D = 1024
NDS = 8


class Buf:
    __slots__ = ("name", "w", "r")

    def __init__(self, name):
        self.name = name
        self.w = None
        self.r = {}


class V:
    __slots__ = ("buf", "ap")

    def __init__(self, buf, ap):
        self.buf = buf
        self.ap = ap

    def __getitem__(self, idx):
        return V(self.buf, self.ap[idx])

    def re(self, s, **kw):
        return V(self.buf, self.ap.rearrange(s, **kw))

    def bc(self, shape):
        return V(self.buf, self.ap.broadcast_to(shape))


class DramT:
    def __init__(self, ap, name):
        self.ap = ap
        self.name = name
        self.bufs = {}

    def v(self, key, idx=None):
        b = self.bufs.get(key)
        if b is None:
            b = self.bufs[key] = Buf(f"{self.name}:{key}")
        ap = self.ap if idx is None else self.ap[idx]
        return V(b, ap)


class Prog:
    def __init__(self):
        self.nc = nc = bass.Bass("TRN2", target_bir_lowering=False)
        self.es = contextlib.ExitStack()
        self.eng = {"pe": nc.tensor, "act": nc.scalar, "dve": nc.vector, "pool": nc.gpsimd, "sp": nc.sync}
        self.semh = {}
        self.cnt = {}
        for e in ("pe", "act", "dve", "pool"):
            self.semh[e] = self.es.enter_context(nc.semaphore("sem_" + e))
            self.cnt[e] = 0
        self.seen = {e: {} for e in self.eng}
        self.dq = {}
        for q in ("sp", "pool", "act"):
            names = []
            for i in range(NDS):
                n = f"d_{q}_{i}"
                self.semh[n] = self.es.enter_context(nc.semaphore(n))
                names.append(n)
            self.dq[q] = dict(names=names, vals=[0] * NDS, i=0)
        self.nps = 0
        self.phase_es = None
        self.bg = None
        self._in_bg = False

    def sb(self, name, shape, dtype=F32):
        es = self.phase_es if self.phase_es is not None else self.es
        self.nps += 1
        name = f"{name}_u{self.nps}"
        h = es.enter_context(self.nc.sbuf_tensor(name, list(shape), dtype))
        return V(Buf(name), h[:])

    @contextlib.contextmanager
    def phase(self):
        self.phase_es = contextlib.ExitStack()
        try:
            yield
        finally:
            self.barrier()
            self.phase_es.close()
            self.phase_es = None

    def barrier(self):
        toks = []
        for q, dq in self.dq.items():
            for n, v in zip(dq["names"], dq["vals"]):
                if v > 0:
                    toks.append((n, v))
        for e in ("pe", "act", "dve", "pool"):
            if self.cnt[e] > 0:
                toks.append((e, self.cnt[e]))
        for e in self.eng:
            for n, v in toks:
                if n != e:
                    self._wait(e, n, v)

    def ps(self, name, shape=(128, 512), dtype=F32):
        h = self.es.enter_context(self.nc.psum_tensor(name, list(shape), dtype))
        return V(Buf(name), h[:])

    def dram(self, name, shape, dtype=F32, kind="Internal"):
        ap = self.nc.dram_tensor(name, list(shape), dtype, kind=kind).ap()
        return DramT(ap, name)

    def _wait(self, e, s, v):
        if self.seen[e].get(s, 0) >= v:
            return
        self.seen[e][s] = v
        self.eng[e].wait_ge(self.semh[s], v)

    def _deps(self, e, reads, writes):
        raw = {}
        oth = {}

        def add(d, tok):
            if tok is None:
                return
            s, v = tok
            if d.get(s, 0) < v:
                d[s] = v

        for b in reads:
            add(raw, b.w)
        for b in writes:
            add(oth, b.w)
            for s, v in b.r.items():
                add(oth, (s, v))
        for s, v in raw.items():
            self._wait(e, s, v)
        for s, v in oth.items():
            if s == e and e == "pe":
                continue
            self._wait(e, s, v)

    def op(self, e, fn, r=(), w=(), inc=True):
        rb = [x.buf for x in r if isinstance(x, V)]
        wb = [x.buf for x in w]
        self._deps(e, rb, wb)
        ins = fn()
        if inc:
            self.cnt[e] += 1
            ins.then_inc(self.semh[e], 1)
            val = self.cnt[e]
        else:
            val = self.cnt[e] + 1
        for b in rb:
            if b.r.get(e, 0) < val:
                b.r[e] = val
        for b in wb:
            b.w = (e, val)
            b.r = {}
        if self.bg is not None and e == "dve" and not self._in_bg:
            self._in_bg = True
            try:
                next(self.bg, None)
            finally:
                self._in_bg = False
        return ins

    def drain_bg(self):
        if self.bg is not None:
            self._in_bg = True
            try:
                for _ in self.bg:
                    pass
            finally:
                self._in_bg = False
            self.bg = None

    def dma(self, q, out, in_):
        dq = self.dq[q]
        i = dq["i"]
        dq["i"] = (i + 1) % NDS
        n = dq["names"][i]
        if dq["vals"][i] > 0:
            self._wait(q, n, dq["vals"][i])
        self._deps(q, [in_.buf], [out.buf])
        dq["vals"][i] += 16
        val = dq["vals"][i]
        self.eng[q].dma_start(out=out.ap, in_=in_.ap).then_inc(self.semh[n], 16)
        if in_.buf.r.get(n, 0) < val:
            in_.buf.r[n] = val
        out.buf.w = (n, val)
        out.buf.r = {}

    def finish(self):
        for q, dq in self.dq.items():
            for n, v in zip(dq["names"], dq["vals"]):
                if v > 0:
                    self._wait("sp", n, v)
        for e in ("pe", "act", "dve", "pool"):
            if self.cnt[e] > 0:
                self._wait("sp", e, self.cnt[e])

    def _a(self, x):
        return x.ap if isinstance(x, V) else x

    def tt(self, e, out, in0, in1, op):
        return self.op(e, lambda: self.eng[e].tensor_tensor(out=out.ap, in0=in0.ap, in1=in1.ap, op=op), r=(in0, in1), w=(out,))

    def ts(self, e, out, in0, s1, s2, op0, op1=None):
        if op1 is None:
            return self.op(e, lambda: self.eng[e].tensor_scalar(out=out.ap, in0=in0.ap, scalar1=self._a(s1), scalar2=None, op0=op0), r=(in0, s1), w=(out,))
        return self.op(e, lambda: self.eng[e].tensor_scalar(out=out.ap, in0=in0.ap, scalar1=self._a(s1), scalar2=self._a(s2), op0=op0, op1=op1), r=(in0, s1, s2), w=(out,))

    def stt(self, out, in0, s, in1, op0, op1):
        return self.op("dve", lambda: self.nc.vector.scalar_tensor_tensor(out=out.ap, in0=in0.ap, scalar=self._a(s), in1=in1.ap, op0=op0, op1=op1), r=(in0, s, in1), w=(out,))

    def act(self, out, in_, func, bias=0.0, scale=1.0):
        return self.op("act", lambda: self.nc.scalar.activation(out=out.ap, in_=in_.ap, func=func, bias=self._a(bias), scale=self._a(scale)), r=(in_, bias, scale), w=(out,))

    def copy(self, e, out, in_):
        if e == "act":
            return self.op(e, lambda: self.nc.scalar.copy(out=out.ap, in_=in_.ap), r=(in_,), w=(out,))
        return self.op(e, lambda: self.eng[e].tensor_copy(out=out.ap, in_=in_.ap), r=(in_,), w=(out,))

    def memset(self, e, out, val):
        return self.op(e, lambda: self.eng[e].memset(out.ap, val), w=(out,))

    def mm(self, out, lhsT, rhs, start=True, stop=True, inc=True):
        return self.op("pe", lambda: self.nc.tensor.matmul(out.ap, lhsT.ap, rhs.ap, start=start, stop=stop), r=(lhsT, rhs), w=(out,), inc=inc)

    def tr(self, out, in_, ident, inc=True):
        return self.op("pe", lambda: self.nc.tensor.transpose(out.ap, in_.ap, ident.ap), r=(in_, ident), w=(out,), inc=inc)

    def recip(self, out, in_):
        return self.op("dve", lambda: self.nc.vector.reciprocal(out=out.ap, in_=in_.ap), r=(in_,), w=(out,))

    def reduce(self, out, in_, op=ALU.add, axis=AX.X):
        return self.op("dve", lambda: self.nc.vector.tensor_reduce(out=out.ap, in_=in_.ap, axis=axis, op=op), r=(in_,), w=(out,))

    def scan(self, out, d0, d1, init, op0, op1):
        return self.op("dve", lambda: self.nc.vector.tensor_tensor_scan(out=out.ap, data0=d0.ap, data1=d1.ap, initial=self._a(init), op0=op0, op1=op1), r=(d0, d1, init), w=(out,))

    def affsel(self, out, in_, pattern, cmp, fill, base, cm):
        return self.op("pool", lambda: self.nc.gpsimd.affine_select(out=out.ap, in_=in_.ap, pattern=pattern, compare_op=cmp, fill=fill, base=base, channel_multiplier=cm), r=(in_,), w=(out,))


WSPEC = {
    "l0_w_in": (1024, 3336), "rwkv_mix": (1792,), "rwkv_w0": (512,), "rwkv_w2": (64, 512), "rwkv_a0": (512,),
    "rwkv_a2": (64, 512), "rwkv_g2": (128, 512), "rwkv_k_k": (512,), "rwkv_k_a": (512,), "rwkv_r_k": (8, 64),
    "rwkv_ln_g": (512,), "rwkv_ln_b": (512,), "ssm_conv_w": (4, 1024), "ssm_conv_b": (1024,), "ssm_dt_bias": (8,),
    "ssm_a_log": (8,), "ssm_d": (8,), "ssm_norm_g": (512,), "l0_w_out": (1024, 1024), "l0_ln1_g": (1024,),
    "l0_ln1_b": (1024,), "ffn0_w_up": (1024, 5632), "ffn0_conv_w": (3, 2816), "ffn0_conv_b": (2816,),
    "ffn0_w_down": (2816, 1024), "l0_ln2_g": (1024,), "l0_ln2_b": (1024,), "l1_w_in": (1024, 1952),
    "mla_q_norm_g": (256,), "mla_w_uq": (256, 768), "mla_kv_norm_g": (128,), "mla_w_ukv": (128, 1024),
    "l1_w_out": (1024, 1024), "l1_ln1_g": (1024,), "l1_ln1_b": (1024,), "ffn1_w_up": (1024, 5632),
    "ffn1_conv_w": (3, 2816), "ffn1_conv_b": (2816,), "ffn1_w_down": (2816, 1024), "l1_ln2_g": (1024,),
    "l1_ln2_b": (1024,),
}


class K:
    def __init__(self, stop_after=None, dbg=()):
        self.P = P = Prog()
        self.stop_after = stop_after
        self.dbg = set(dbg)
        self.W = {}
        for n, shp in WSPEC.items():
            s2 = list(shp) if len(shp) == 2 else [1, shp[0]]
            self.W[n] = P.dram(n, s2, F32, kind="ExternalInput")
        self.x = P.dram("x", [NT, D], F32, kind="ExternalInput")
        self.pos = P.dram("positions", [NB, T], I32, kind="ExternalInput")
        self.out = P.dram("out", [NT, D], F32, kind="ExternalOutput")
        self.rope_c = P.dram("rope_c", [32, 2], F32, kind="ExternalInput")
        self.ps = [P.ps(f"psb{i}") for i in range(8)]
        self.consts()

    def scratch(self, name, shape, dtype=F32):
        kind = "ExternalOutput" if name in self.dbg else "Internal"
        return self.P.dram(name, shape, dtype, kind=kind)

    def consts(self):
        P = self.P
        self.ones = P.sb("ones", [128, 128])
        P.memset("pool", self.ones, 1.0)
        self.ident = P.sb("ident", [128, 128])
        P.affsel(self.ident, self.ones, [[-1, 128]], ALU.is_equal, 0.0, 0, 1)
        self.identb = P.sb("identb", [128, 128], BF16)
        P.copy("pool", self.identb, self.ident)
        self.onesb = P.sb("onesb", [128, 128], BF16)
        P.copy("pool", self.onesb, self.ones)

    def col_load(self, q, dst, src_dt, idx, n, p):
        P = self.P
        src = V(src_dt.v("c").buf, src_dt.ap[0, idx].rearrange("(j p) -> p j", p=p))
        with P.nc.allow_non_contiguous_dma(reason="small per-channel vector"):
            P.dma(q, dst, src)

    def load_hT(self, src, hT, halo, bsel=None):
        P = self.P
        if not hasattr(self, "_ld_xt") or self._ld_xt_phase is not P.phase_es:
            self._ld_xt = [P.sb(f"ld_xt{i}", [128, D]) for i in range(2)]
            self._ld_xt_phase = P.phase_es
        xt = self._ld_xt
        n = 0
        for b_src in (range(NB) if bsel is None else bsel):
            b = b_src if bsel is None else 0
            if halo:
                P.memset("pool", hT[:, b, :, 0:halo], 0.0)
            for tt in range(T // 128):
                x_ = xt[n % 2]
                r0 = b_src * T + tt * 128
                P.dma("sp", x_, src.v(("t", b_src, tt), (slice(r0, r0 + 128), slice(None))))
                for half in range(2):
                    ps = self.ps[(2 * n + half) % 4]
                    for kk in range(4):
                        kc = half * 4 + kk
                        P.tr(ps[:, kk * 128:(kk + 1) * 128], x_[:, kc * 128:(kc + 1) * 128], self.ident, inc=(kk == 3))
                    eng = "act" if half == 0 else "dve"
                    P.copy(eng, hT[:, b, half * 4:half * 4 + 4, halo + tt * 128: halo + (tt + 1) * 128],
                           ps.re("p (a c) -> p a c", a=4))
                n += 1

    def load_w(self, wdt, k0, kn, c0, cn, dst, stage):
        P = self.P
        src = wdt.v("w", (slice(k0 * 128, (k0 + kn) * 128), slice(c0, c0 + cn))).re("(k p) c -> p k c", p=128)
        P.dma("sp", stage[:, 0:kn, 0:cn], src)
        P.copy("act", dst, stage[:, 0:kn, 0:cn])

    def inproj(self, hT, halo, wdt, col_tiles, PT):
        P = self.P
        wst = [P.sb(f"ip_wst{i}", [128, 8, 128]) for i in range(2)]
        wb = [P.sb(f"ip_wb{i}", [128, 8, 128], BF16) for i in range(2)]
        ost = [P.sb(f"ip_o{i}", [128, T]) for i in range(2)]
        n = 0
        c0_, cn_ = col_tiles[0]
        self.load_w(wdt, 0, 8, c0_, cn_, wb[0][:, :, 0:cn_], wst[0])
        for ci, (c0, cn) in enumerate(col_tiles):
            w_ = wb[ci % 2]
            if ci + 1 < len(col_tiles):
                c0_, cn_ = col_tiles[ci + 1]
                self.load_w(wdt, 0, 8, c0_, cn_, wb[(ci + 1) % 2][:, :, 0:cn_], wst[(ci + 1) % 2])
            for b in range(NB):
                o = ost[n % 2]
                for tb in range(4):
                    ps = self.ps[4 + tb]
                    for k in range(8):
                        P.mm(ps[0:cn, :], w_[:, k, 0:cn], hT[:, b, k, halo + tb * 512: halo + (tb + 1) * 512],
                             start=(k == 0), stop=(k == 7), inc=(k == 7))
                    P.copy("act" if tb % 2 == 0 else "dve", o[0:cn, tb * 512:(tb + 1) * 512], ps[0:cn, :])
                P.dma("pool", PT.v(("c", ci, b), (slice(c0, c0 + cn), slice(b * T, (b + 1) * T))), o[0:cn, :])
                n += 1

    def rwkv(self, PT0, YM, VTM, BON):
        P = self.P
        W = self.W
        C = 64
        NCH = T // C
        sb = P.sb
        def cols(name, src, off, n, p):
            t = sb(name, [p, n])
            self.col_load("sp", t, src, slice(off, off + n * p), n, p)
            return t
        mix_r = cols("rw_mixr", W["rwkv_mix"], 0, 8, 64)
        mix_k = cols("rw_mixk", W["rwkv_mix"], 512, 8, 64)
        mix_v = cols("rw_mixv", W["rwkv_mix"], 1024, 8, 64)
        mix_w = cols("rw_mixw", W["rwkv_mix"], 1536, 1, 64)
        mix_a = cols("rw_mixa", W["rwkv_mix"], 1600, 1, 64)
        mix_g = cols("rw_mixg", W["rwkv_mix"], 1664, 1, 128)
        w0 = cols("rw_w0", W["rwkv_w0"], 0, 8, 64)
        a0 = cols("rw_a0", W["rwkv_a0"], 0, 8, 64)
        k_k = cols("rw_kk", W["rwkv_k_k"], 0, 8, 64)
        k_a = cols("rw_ka", W["rwkv_k_a"], 0, 8, 64)
        omk = sb("rw_omk", [64, 8])
        P.ts("dve", omk, k_a, -1.0, 1.0, ALU.mult, ALU.add)
        rk = sb("rw_rk", [64, 8])
        with P.nc.allow_non_contiguous_dma(reason="small"):
            P.dma("sp", rk, V(W["rwkv_r_k"].v("c").buf, W["rwkv_r_k"].ap.rearrange("h j -> j h")))
        w2 = sb("rw_w2", [64, 512])
        P.dma("sp", w2, W["rwkv_w2"].v("c"))
        a2 = sb("rw_a2", [64, 512])
        P.dma("sp", a2, W["rwkv_a2"].v("c"))
        g2 = sb("rw_g2", [128, 512])
        P.dma("sp", g2, W["rwkv_g2"].v("c"))
        lng = sb("rw_lng", [128, 512])
        P.dma("sp", lng, V(W["rwkv_ln_g"].v("c").buf, W["rwkv_ln_g"].ap[0, :].partition_broadcast(128)))
        lnb = sb("rw_lnb", [128, 512])
        P.dma("sp", lnb, V(W["rwkv_ln_b"].v("c").buf, W["rwkv_ln_b"].ap[0, :].partition_broadcast(128)))
        o64 = sb("rw_o64", [64, 512], BF16)
        P.memset("pool", o64, 1.0)
        msu = sb("rw_msu", [64, 512], BF16)
        mu = sb("rw_mu", [64, 512], BF16)
        msl = sb("rw_msl", [64, 512], BF16)
        i8 = sb("rw_i8", [64, 512])
        pat = [[0, 8], [1, 64]]
        P.affsel(msu.re("p (a b) -> p a b", a=8), o64.re("p (a b) -> p a b", a=8), pat, ALU.is_gt, 0.0, 0, -1)
        P.affsel(mu.re("p (a b) -> p a b", a=8), o64.re("p (a b) -> p a b", a=8), pat, ALU.is_ge, 0.0, 0, -1)
        P.affsel(msl.re("p (a b) -> p a b", a=8), o64.re("p (a b) -> p a b", a=8), [[0, 8], [-1, 64]], ALU.is_gt, 0.0, 0, 1)
        P.affsel(i8.re("p (a b) -> p a b", a=8), o64.re("p (a b) -> p a b", a=8), pat, ALU.is_equal, 0.0, 0, -1)
        m01 = sb("rw_m01", [64, T], BF16)
        P.memset("pool", m01, 1.0)
        P.memset("pool", m01.re("p (c t) -> p c t", t=C)[:, :, 0:1], 0.0)
        X0 = sb("rw_X0", [64, T + 1]); X1 = sb("rw_X1", [64, T + 1]); X2 = sb("rw_X2", [64, T + 1])
        for x_ in (X0, X1, X2):
            P.memset("pool", x_[:, 0:1], 0.0)
        TMPG = sb("rw_TMPG", [128, T]); TMP = TMPG[0:64]; B3 = sb("rw_B3", [64, T]); B4 = sb("rw_B4", [64, T]); B5 = sb("rw_B5", [64, T])
        B6 = sb("rw_B6", [64, T]); B7 = sb("rw_B7", [64, T]); E = sb("rw_E", [64, T])
        LW = sb("rw_LW", [64, T + 1]); LA = sb("rw_LA", [64, T + 1]); LG = sb("rw_LG", [128, T + 1])
        for x_ in (LW, LA, LG):
            P.memset("pool", x_[:, 0:1], 0.0)
        GC = sb("rw_GC", [64, NCH]); CUMC = sb("rw_CUMC", [64, NCH]); BONS = sb("rw_BONS", [64, NCH])
        VM2 = [sb(f"rw_VM2{i}", [64, 512]) for i in range(2)]; Y1_all = sb("rw_Y1all", [64, T]); MT_all = sb("rw_MTall", [64, T])
        D0_all = sb("rw_D0all", [64, T]); RH_all = sb("rw_RHall", [64, T]); S_all = sb("rw_Sall", [64, T + 64])
        Y_all = sb("rw_Yall", [64, T])
        yt = [TMPG[:, i * 512:(i + 1) * 512] for i in range(2)]
        vt = [TMPG[:, (2 + i) * 512:(3 + i) * 512] for i in range(2)]
        bo = [sb(f"rw_bo{i}", [128, 8]) for i in range(2)]
        st = [sb(f"rw_st{i}", [128, 8]) for i in range(2)]
        sq = [sb("rw_sq0", [128, 512])] * 2
        nm = ["AM", "BME", "KME", "AAB", "ARB", "AAK", "ARK", "PA", "PTA", "PB", "PTB", "R", "AH", "XX1", "U0T"]
        bt = {n_: sb("rw_" + n_, [64, 512], BF16) for n_ in nm}
        opb = {n_: sb("rw_ob_" + n_, [64, T], BF16) for n_ in ("RT", "AT")}
        b3b = V(B3.buf, B3.ap.bitcast(BF16))
        b4b = V(B4.buf, B4.ap.bitcast(BF16))
        opb["BT"] = b3b[:, 0:T]; opb["KT"] = b3b[:, T:2 * T]
        opb["BTE"] = b4b[:, 0:T]; opb["KTE"] = b4b[:, T:2 * T]
        VMb = sb("rw_VMb", [64, 512], BF16)
        DG32 = sb("rw_DG32", [64, 512])
        i8b = sb("rw_i8b", [64, 512], BF16)
        P.copy("pool", i8b, i8)
        P.memset("pool", S_all[:, 0:64], 0.0)
        ps = self.ps

        def shift(raw, mixcol, tmp):
            p = raw.ap.shape[0]
            P.tt("dve", tmp, raw[:, 0:T], raw[:, 1:T + 1], ALU.subtract)
            P.stt(raw[:, 1:T + 1], tmp, mixcol, raw[:, 1:T + 1], ALU.mult, ALU.add)

        def c3(v_):
            return v_.re("p (a b) -> p a b", b=64)

        for b in range(NB):
            tsl = slice(b * T, (b + 1) * T)
            P.dma("sp", LW[:, 1:], PT0.v(("c", 12, b), (slice(1536, 1600), tsl)))
            P.dma("sp", LA[:, 1:], PT0.v(("c", 12, b), (slice(1600, 1664), tsl)))
            P.dma("sp", LG[:, 1:], PT0.v(("c", 13, b), (slice(1664, 1792), tsl)))
            shift(LW, mix_w[:, 0:1], TMP)
            shift(LA, mix_a[:, 0:1], TMP)
            shift(LG, mix_g[:, 0:1], TMPG)
            P.act(LW[:, 1:], LW[:, 1:], AF.Tanh)
            P.act(LG[:, 1:], LG[:, 1:], AF.Sigmoid)
            if self.stop_after == "rw_lora":
                return
            for h in range(8):
                hs = slice(h * 64, (h + 1) * 64)
                hc = slice(h, h + 1)
                P.dma("sp", X0[:, 1:], PT0.v(("c", h // 2, b), (slice(h * 64, h * 64 + 64), tsl)))
                P.dma("sp", X1[:, 1:], PT0.v(("c", 4 + h // 2, b), (slice(512 + h * 64, 512 + h * 64 + 64), tsl)))
                P.dma("sp", X2[:, 1:], PT0.v(("c", 8 + h // 2, b), (slice(1024 + h * 64, 1024 + h * 64 + 64), tsl)))
                shift(X0, mix_r[:, hc], TMP)
                shift(X1, mix_k[:, hc], TMP)
                shift(X2, mix_v[:, hc], TMP)
                R_ = X0[:, 1:]; K_ = X1[:, 1:]; V_ = X2[:, 1:]
                for tb in range(4):
                    bs = slice(tb * 512, (tb + 1) * 512)
                    P.mm(ps[0][0:64, :], w2[:, hs], LW[:, 1 + tb * 512: 1 + (tb + 1) * 512])
                    P.act(B3[:, bs], ps[0][0:64, :], AF.Sigmoid, bias=w0[:, hc])
                    P.mm(ps[1][0:64, :], a2[:, hs], LA[:, 1 + tb * 512: 1 + (tb + 1) * 512])
                    P.act(B4[:, bs], ps[1][0:64, :], AF.Sigmoid, bias=a0[:, hc])
                LD = B3; A_ = B4
                P.ts("pool", LD, LD, -math.exp(-0.5), 0.0, ALU.mult, ALU.add)
                KK = B5
                P.act(KK, K_, AF.Copy, scale=k_k[:, hc])
                P.act(TMP, KK, AF.Square)
                for tb in range(4):
                    bs = slice(tb * 512, (tb + 1) * 512)
                    P.mm(ps[2 + tb % 2][0:64, :], self.ones[0:64, 0:64], TMP[:, bs])
                    P.ts("dve", E[:, bs], ps[2 + tb % 2][0:64, :], 1e-24, None, ALU.max)
                P.act(E, E, AF.Ln)
                P.act(E, E, AF.Exp, scale=-0.5)
                P.tt("dve", KK, KK, E, ALU.mult)
                KM = B6
                P.act(TMP, A_, AF.Identity, bias=omk[:, hc], scale=k_a[:, hc])
                P.tt("dve", KM, TMP, K_, ALU.mult)
                P.tt("dve", TMP, R_, KM, ALU.mult)
                for c in range(NCH):
                    P.mm(ps[4][0:64, c:c + 1], TMP[:, c * C:(c + 1) * C], rk[:, hc], inc=(c == NCH - 1))
                P.copy("act", BONS, ps[4][0:64, 0:NCH])
                with P.nc.allow_non_contiguous_dma(reason="small"):
                    P.dma("pool", BON.v(("bh", b, h), (slice(b * T, (b + 1) * T), slice(h, h + 1))).re("(c t) o -> t (c o)", t=C), BONS)
                CUM = B7
                P.scan(CUM, m01, LD, 0.0, ALU.mult, ALU.add)
                P.copy("pool", CUMC, CUM.re("p (c t) -> p c t", t=C)[:, :, C - 1])
                P.act(E, CUM, AF.Exp)
                P.copy("pool", GC, E.re("p (c t) -> p c t", t=C)[:, :, C - 1])
                P.tt("dve", R_, R_, E, ALU.mult)
                P.copy("act", opb["RT"], R_)
                P.tt("dve", TMP, CUM, LD, ALU.subtract)
                P.act(E, TMP, AF.Exp)
                P.stt(opb["AT"], KK, -1.0, E, ALU.mult, ALU.mult)
                P.tt("dve", KK, KK, A_, ALU.mult)
                P.act(E, CUM, AF.Exp, scale=-1.0)
                P.tt("dve", opb["BT"], KK, E, ALU.mult)
                P.tt("dve", opb["KT"], KM, E, ALU.mult)
                P.tt("pool", c3(TMP), CUMC.re("p (c o) -> p c o", o=1).bc([64, NCH, C]), c3(CUM), ALU.subtract)
                P.act(E, TMP, AF.Exp)
                P.tt("dve", opb["BTE"], KK, E, ALU.mult)
                P.tt("dve", opb["KTE"], KM, E, ALU.mult)
                RT32, VT = X0[:, 1:], X2[:, 1:]
                RT, AT, KT, BT, BTE, KTE = (opb[n_] for n_ in ("RT", "AT", "KT", "BT", "BTE", "KTE"))
                P.drain_bg()
                if self.stop_after == "rw_prep":
                    return
                for cb in range(NCH // 8):
                    bs = slice(cb * 512, (cb + 1) * 512)

                    def cs(v_, c):
                        return v_[:, cb * 512 + c * 64: cb * 512 + (c + 1) * 64]

                    def bsl(v_, c):
                        return v_[:, c * 64:(c + 1) * 64]

                    def mm8(pst, lf, rf, second=None):
                        for c in range(8):
                            if second is None:
                                P.mm(bsl(pst, c)[0:64], lf(c), rf(c), inc=(c == 7))
                            else:
                                P.mm(bsl(pst, c)[0:64], lf(c), rf(c), start=True, stop=False, inc=False)
                                P.mm(bsl(pst, c)[0:64], second[0](c), second[1](c), start=False, stop=True, inc=(c == 7))
                    vm32 = VM2[cb % 2]
                    for i_, (src, dst) in enumerate(((AT, bt["AM"]), (BTE, bt["BME"]), (KTE, bt["KME"]), (VT, vm32))):
                        for c in range(8):
                            if i_ < 3:
                                P.mm(bsl(ps[i_], c)[0:64], cs(src, c), self.identb[0:64, 0:64], inc=(c == 7))
                            else:
                                P.tr(bsl(ps[i_], c)[0:64], cs(src, c), self.ident[0:64, 0:64], inc=(c == 7))
                        P.copy("act" if i_ % 2 == 0 else "dve", dst, ps[i_][0:64, :])
                    P.copy("act", VMb, vm32)
                    P.dma("pool", VTM.v(("rw", b, h, cb), (slice(b * T + cb * 512, b * T + (cb + 1) * 512), hs)).re("(c t) i -> t c i", t=C), c3(vm32))
                    VM = VMb
                    mm8(ps[4], lambda c: cs(BT, c), lambda c: cs(AT, c))
                    P.tt("dve", bt["AAB"], ps[4][0:64, :], msu, ALU.mult)
                    mm8(ps[5], lambda c: cs(BT, c), lambda c: cs(RT, c))
                    P.tt("dve", bt["ARB"], ps[5][0:64, :], mu, ALU.mult)
                    mm8(ps[6], lambda c: cs(KT, c), lambda c: cs(AT, c))
                    P.tt("dve", bt["AAK"], ps[6][0:64, :], msu, ALU.mult)
                    mm8(ps[7], lambda c: cs(KT, c), lambda c: cs(RT, c))
                    P.tt("dve", bt["ARK"], ps[7][0:64, :], mu, ALU.mult)
                    mm8(ps[0], lambda c: cs(AT, c), lambda c: cs(BT, c))
                    P.tt("dve", bt["PTA"], ps[0][0:64, :], msl, ALU.mult)
                    P.tt("pool", bt["R"], bt["AAB"], i8b, ALU.add)
                    Pc, PTc = bt["AAB"], bt["PTA"]
                    nxt = [(bt["PB"], bt["PTB"]), (bt["PA"], bt["PTA"])]
                    for lvl in range(5):
                        Pn, PTn = nxt[lvl % 2]
                        if lvl < 4:
                            mm8(ps[1], lambda c: bsl(PTc, c), lambda c: bsl(Pc, c))
                            P.copy("act", Pn, ps[1][0:64, :])
                        mm8(ps[2], lambda c: bsl(Pc, c), lambda c: bsl(PTc, c))
                        P.copy("act", PTn, ps[2][0:64, :])
                        mm8(ps[3], lambda c: bsl(PTn, c), lambda c: bsl(bt["R"], c))
                        P.tt("dve", bt["R"], bt["R"], ps[3][0:64, :], ALU.add)
                        Pc, PTc = Pn, PTn
                    Ti = bt["R"]
                    mm8(ps[4], lambda c: bsl(Ti, c), lambda c: bsl(bt["AM"], c))
                    P.copy("act", bt["AH"], ps[4][0:64, :])
                    mm8(ps[5], lambda c: bsl(bt["AAK"], c), lambda c: bsl(VM, c))
                    P.copy("act", bt["XX1"], ps[5][0:64, :])
                    mm8(ps[6], lambda c: bsl(Ti, c), lambda c: bsl(bt["XX1"], c))
                    P.copy("act", bt["U0T"], ps[6][0:64, :])
                    mm8(ps[7], lambda c: bsl(bt["AH"], c), lambda c: bsl(bt["ARB"], c))
                    P.tt("dve", RH_all[:, bs], ps[7][0:64, :], RT32[:, bs], ALU.add)
                    mm8(ps[0], lambda c: bsl(bt["ARB"], c), lambda c: bsl(bt["U0T"], c),
                        second=(lambda c: bsl(bt["ARK"], c), lambda c: bsl(VM, c)))
                    P.copy("act", Y1_all[:, bs], ps[0][0:64, :])
                    mm8(ps[1], lambda c: bsl(bt["AH"], c), lambda c: bsl(bt["BME"], c))
                    P.tt("pool", c3(DG32), c3(i8), GC[:, cb * 8:(cb + 1) * 8].re("p (c o) -> p c o", o=1).bc([64, 8, C]), ALU.mult)
                    P.tt("dve", MT_all[:, bs], ps[1][0:64, :], DG32, ALU.add)
                    mm8(ps[2], lambda c: bsl(bt["BME"], c), lambda c: bsl(bt["U0T"], c),
                        second=(lambda c: bsl(bt["KME"], c), lambda c: bsl(VM, c)))
                    P.copy("act", D0_all[:, bs], ps[2][0:64, :])
                if self.stop_after == "rw_pre":
                    return

                def tail_gen(b=b, h=h, tsl=tsl, hs=hs):
                    for c in range(NCH):
                        pss = ps[6 + c % 2]
                        P.mm(pss[0:64, 0:64], MT_all[:, c * 64:(c + 1) * 64], S_all[:, c * 64:(c + 1) * 64])
                        P.tt("dve", S_all[:, (c + 1) * 64:(c + 2) * 64], pss[0:64, 0:64], D0_all[:, c * 64:(c + 1) * 64], ALU.add)
                        yield
                    for cb in range(NCH // 8):
                        bs = slice(cb * 512, (cb + 1) * 512)
                        pst = ps[5]
                        for c in range(8):
                            cc = cb * 8 + c
                            P.mm(pst[0:64, c * 64:(c + 1) * 64], RH_all[:, cc * 64:(cc + 1) * 64], S_all[:, cc * 64:(cc + 1) * 64], inc=(c == 7))
                        P.tt("dve", Y_all[:, bs], pst[0:64, :], Y1_all[:, bs], ALU.add)
                        yield
                    P.dma("pool", YM.v(("rw", b, h), (tsl, hs)).re("(c t) i -> t c i", t=C), c3(Y_all))
                    yield

                P.bg = tail_gen()
                if self.stop_after in ("rw_seq", "rw_y"):
                    P.drain_bg()
                    return
            P.drain_bg()
            P.barrier()
            def h3(v_):
                return v_.re("p (h i) -> p h i", i=64)
            def hb(v_):
                return v_.re("p (h o) -> p h o", o=1).bc([128, 8, 64])
            for tt in range(T // 128):
                r0 = b * T + tt * 128
                rs = slice(r0, r0 + 128)
                y_ = yt[tt % 2]; v_ = vt[tt % 2]; b_ = bo[tt % 2]; s_ = st[tt % 2]; q_ = sq[tt % 2]
                for h in range(8):
                    pass
                P.dma("sp", y_, V(YM.v(("rw", b, 0)).buf, YM.ap[rs, 0:512]))
                P.dma("sp", v_, V(VTM.v(("rw", b, 0)).buf, VTM.ap[rs, 0:512]))
                P.dma("sp", b_, V(BON.v(("bh", b, 0)).buf, BON.ap[rs, :]))
                pg = ps[tt % 2]
                P.mm(pg, LG[:, 1 + tt * 128: 1 + (tt + 1) * 128], g2)
                P.reduce(s_, h3(y_))
                P.ts("dve", s_, s_, -1.0 / 64, None, ALU.mult)
                P.tt("dve", h3(y_), h3(y_), hb(s_), ALU.add)
                P.act(q_, y_, AF.Square)
                P.reduce(s_, h3(q_))
                P.ts("dve", s_, s_, 1.0 / 64, 64e-5, ALU.mult, ALU.add)
                P.act(s_, s_, AF.Sqrt)
                P.recip(s_, s_)
                P.tt("dve", h3(y_), h3(y_), hb(s_), ALU.mult)
                P.tt("dve", y_, y_, lng, ALU.mult)
                P.tt("dve", y_, y_, lnb, ALU.add)
                P.tt("dve", h3(v_), h3(v_), hb(b_), ALU.mult)
                P.tt("dve", y_, y_, v_, ALU.add)
                P.tt("dve", y_, y_, pg, ALU.mult)
                P.dma("pool", V(YM.v(("rwo", b, tt)).buf, YM.ap[rs, 0:512]), y_)

    def layer0_mixer(self):
        P = self.P
        self.PT0 = self.scratch("PT0", [3336, NT])
        self.YM = self.scratch("YM", [NT, D])
        self.VTM = self.scratch("VTM", [NT, 512])
        self.BON = self.scratch("BON", [NT, 8])
        with P.phase():
            hT = P.sb("hT", [128, NB, 8, T], BF16)
            self.load_hT(self.x, hT, 0)
            tiles = [(i * 128, 128) for i in range(26)] + [(3328, 8)]
            self.inproj(hT, 0, self.W["l0_w_in"], tiles, self.PT0)
        if self.stop_after == "inproj0":
            return
        import os as _os
        if not _os.environ.get("SKIP_RWKV"):
            with P.phase():
                self.rwkv(self.PT0, self.YM, self.VTM, self.BON)
        if self.stop_after == "rwkv":
            return
        self.YBTd = self.scratch("YBTd", [128, NB * 4 * T], BF16)
        with P.phase():
            YBT = P.sb("mb_YBT", [128, NB, 4, T], BF16)
            self.mamba(self.PT0, YBT)
            P.dma("pool", self.YBTd.v("all"), YBT.re("p b j t -> p (b j t)"))
        if self.stop_after == "mamba":
            return
        self.H1 = self.scratch("H1", [NT, D])
        self.H2 = self.scratch("H2", [NT, D])
        with P.phase():
            YBT = P.sb("op_YBT", [128, NB, 4, T], BF16)
            P.dma("sp", YBT.re("p b j t -> p (b j t)"), self.YBTd.v("all"))
            self.outproj_ln(self.YM, YBT, "l0_w_out", "l0_ln1_g", "l0_ln1_b", self.x, self.H1)
        if self.stop_after == "h1":
            return
        with P.phase():
            self.ffn(self.H1, self.H2, "l0_ln2", "ffn0")
        if self.stop_after == "h2":
            return


def _in_map(inputs, c):
    m = {}
    for n, shp in WSPEC.items():
        a = np.ascontiguousarray(inputs[n], dtype=np.float32)
        m[n] = a if a.ndim == 2 else a.reshape(1, -1)
    m["x"] = np.ascontiguousarray(inputs["x"][2 * c:2 * c + 2]).reshape(NT, D)
    m["positions"] = np.ascontiguousarray(inputs["positions"][2 * c:2 * c + 2]).astype(np.int32)
    invf = (1.0 / (np.float32(10000.0) ** (np.arange(0, 32, 2, dtype=np.float32) / np.float32(32)))).astype(np.float32)
    rc = np.zeros((32, 2), np.float32)
    rc[:, 0] = np.concatenate([invf, invf])
    rc[:16, 1] = -1.0
    rc[16:, 1] = 1.0
    m["rope_c"] = rc
    return m


def _mamba(self, PT0, YBT):
    P = self.P
    W = self.W
    sb = P.sb
    ps = self.ps
    L = 128
    OFF = 1792
    cw = []
    for i in range(4):
        t = sb(f"mb_cw{i}", [128, 8])
        with P.nc.allow_non_contiguous_dma(reason="small"):
            P.dma("sp", t, V(W["ssm_conv_w"].v("c").buf, W["ssm_conv_w"].ap[i, :].rearrange("(j p) -> p j", p=128)))
        cw.append(t)
    cb_ = sb("mb_cb", [128, 8])
    self.col_load("sp", cb_, W["ssm_conv_b"], slice(0, 1024), 8, 128)
    ng = sb("mb_ng", [128, 4])
    self.col_load("sp", ng, W["ssm_norm_g"], slice(0, 512), 4, 128)
    dtb = sb("mb_dtb", [8, 1])
    self.col_load("sp", dtb, W["ssm_dt_bias"], slice(0, 8), 1, 8)
    alog = sb("mb_alog", [8, 1])
    self.col_load("sp", alog, W["ssm_a_log"], slice(0, 8), 1, 8)
    P.act(alog, alog, AF.Exp)
    P.ts("dve", alog, alog, -1.0, None, ALU.mult)
    dsk = sb("mb_dsk", [128, 4])
    with P.nc.allow_non_contiguous_dma(reason="small"):
        dv = W["ssm_d"].ap[0, 0:8].rearrange("(j two) -> two j", two=2)
        P.dma("sp", dsk[0:64, :], V(W["ssm_d"].v("c").buf, dv[0].partition_broadcast(64)))
        P.dma("sp", dsk[64:128, :], V(W["ssm_d"].v("c").buf, dv[1].partition_broadcast(64)))
    tri = sb("mb_tri", [128, 128])
    P.affsel(tri, self.ones, [[1, 128]], ALU.is_ge, 0.0, 0, -1)
    ZT = sb("mb_ZT", [128, 4, T])
    XC = sb("mb_XC", [128, 8, T])
    RAW = [sb(f"mb_RAW{i}", [128, 3 + T]) for i in range(1)]
    for r_ in RAW:
        P.memset("pool", r_[:, 0:3], 0.0)
    DTr = sb("mb_DT", [8, T]); ADT = sb("mb_ADT", [8, T])
    BTb = sb("mb_BTb", [128, 2, T], BF16); CTb = sb("mb_CTb", [128, 2, T], BF16)
    S = sb("mb_S", [128, 8, 64]); STp = sb("mb_STp", [128, 8, 128], BF16)
    Xp = [sb(f"mb_Xp{i}", [128, 8, 128], BF16) for i in range(2)]
    for x_ in Xp:
        P.memset("pool", x_, 0.0)
    Xtm = sb("mb_Xtm", [128, 8, 64]); Xd = sb("mb_Xd", [128, 8, 64], BF16)
    Btm = sb("mb_Btm", [128, 256], BF16)
    DTA = sb("mb_DTA", [128, 16]); CS = sb("mb_CS", [128, 8]); NCS = sb("mb_NCS", [128, 8])
    CBT = sb("mb_CBT", [128, 256]); ECE = sb("mb_ECE", [128, 8]); DECE = sb("mb_DECE", [128, 8])
    ABC4 = [sb(f"mb_ABC{i}", [128, 128]) for i in range(4)]
    DIF4 = [sb(f"mb_DIF{i}", [128, 512]) for i in range(2)]
    MTf4 = [sb(f"mb_MTf{i}", [128, 512]) for i in range(2)]
    MTb4 = [sb(f"mb_MTb{i}", [128, 512], BF16) for i in range(2)]
    ECU = sb("mb_ECU", [128, 4, 128])
    YT = sb("mb_YT", [128, 128]); SQ = sb("mb_SQ", [128, 2, 512]); RS = sb("mb_RS", [128, 512])
    for b in range(NB):
        tsl = slice(b * T, (b + 1) * T)
        P.memset("pool", S, 0.0)
        P.memset("pool", STp, 0.0)
        for j in range(4):
            P.dma("sp", ZT[:, j, :], PT0.v(("c", 14 + j, b), (slice(OFF + j * 128, OFF + (j + 1) * 128), tsl)))
            P.act(ZT[:, j, :], ZT[:, j, :], AF.Silu)
        for j in range(8):
            r_ = RAW[0]
            c0 = OFF + 512 + j * 128
            P.dma("sp", r_[:, 3:], PT0.v(("c", 18 + j, b), (slice(c0, c0 + 128), tsl)))
            acc = XC[:, j, :]
            P.ts("dve", acc, r_[:, 0:T], cw[0][:, j:j + 1], cb_[:, j:j + 1], ALU.mult, ALU.add)
            for i in range(1, 4):
                P.stt(acc, r_[:, i:i + T], cw[i][:, j:j + 1], acc, ALU.mult, ALU.add)
            P.act(acc, acc, AF.Silu)
        P.dma("sp", DTr, PT0.v(("c", 26, b), (slice(3328, 3336), tsl)))
        P.act(DTr, DTr, AF.Exp, bias=dtb[:, 0:1])
        P.act(DTr, DTr, AF.Ln, bias=1.0)
        P.ts("dve", ADT, DTr, alog[:, 0:1], None, ALU.mult)
        for g in range(2):
            P.copy("pool", BTb[:, g, :], XC[:, 4 + g, :])
            P.copy("pool", CTb[:, g, :], XC[:, 6 + g, :])
        for cc in range(T // L):
            csl = slice(cc * L, (cc + 1) * L)
            xp = Xp[cc % 2]
            for j in range(4):
                P.tr(ps[0][:, j * 128:(j + 1) * 128], XC[:, j, csl], self.ident, inc=(j == 3))
            for g in range(2):
                P.tr(ps[1][:, g * 128:(g + 1) * 128], XC[:, 4 + g, csl], self.ident, inc=False)
            P.tr(ps[1][:, 256:264], DTr[:, csl], self.ident[0:8, 0:8], inc=False)
            P.tr(ps[1][:, 264:272], ADT[:, csl], self.ident[0:8, 0:8], inc=True)
            P.copy("act", DTA, ps[1][:, 256:272])
            P.copy("act", Btm, ps[1][:, 0:256])
            P.tt("dve", Xtm, ps[0].re("p (h i) -> p h i", i=64), DTA[:, 0:8].re("p (h o) -> p h o", o=1).bc([128, 8, 64]), ALU.mult)
            P.copy("pool", xp[:, 0:8:2, 0:64], Xtm[:, 0:8:2, :])
            P.copy("pool", xp[:, 1:8:2, 64:128], Xtm[:, 1:8:2, :])
            P.mm(ps[2][:, 0:8], tri, DTA[:, 8:16])
            P.copy("act", CS, ps[2][:, 0:8])
            P.ts("dve", NCS, CS, -1.0, None, ALU.mult)
            for g in range(2):
                P.mm(ps[3][:, g * 128:(g + 1) * 128], BTb[:, g, csl], CTb[:, g, csl])
            P.copy("act", CBT, ps[3][:, 0:256])
            pcs = [ps[4], ps[5]]
            for g in range(2):
                for hh in range(4):
                    h = 4 * g + hh
                    P.ts("pool", ABC4[hh], self.ones, DTA[:, 8 + h:9 + h], 0.0, ALU.mult, ALU.add)
                    P.mm(pcs[g][:, hh * 128:(hh + 1) * 128], ABC4[hh], tri, inc=(hh == 3))
            for g in range(2):
                pc3 = pcs[g].re("p (h l) -> p h l", l=128)
                for hh in range(4):
                    h = 4 * g + hh
                    P.ts("dve", DIF4[g][:, hh * 128:(hh + 1) * 128], pcs[g][:, hh * 128:(hh + 1) * 128], CS[:, h:h + 1], 0.0, ALU.subtract, ALU.min)
                P.act(DIF4[g], DIF4[g], AF.Exp)
                P.act(ECE[:, 4 * g:4 * g + 4], pc3[:, :, 127], AF.Exp)
                P.tt("dve", DECE[:, 4 * g:4 * g + 4], pc3[:, :, 127], CS[:, 4 * g:4 * g + 4], ALU.subtract)
                P.act(DECE[:, 4 * g:4 * g + 4], DECE[:, 4 * g:4 * g + 4], AF.Exp)
                P.act(ECU[0:64, 2 * g:2 * g + 2, :], pc3[0:64, 0:4:2, :], AF.Exp)
                P.act(ECU[64:128, 2 * g:2 * g + 2, :], pc3[64:128, 1:4:2, :], AF.Exp)
                d3 = DIF4[g].re("p (h l) -> p h l", l=128)
                P.tt("dve", MTf4[g].re("p (h l) -> p h l", l=128), CBT[:, g * 128:(g + 1) * 128].re("p (o l) -> p o l", o=1).bc([128, 4, 128]), d3, ALU.mult)
                P.tt("pool", MTb4[g].re("p (h l) -> p h l", l=128), MTf4[g].re("p (h l) -> p h l", l=128), tri.re("p (o l) -> p o l", o=1).bc([128, 4, 128]), ALU.mult)
            for h in range(8):
                g = h // 4
                hh = h % 4
                pr = h // 2
                P.mm(ps[6][:, pr * 128:(pr + 1) * 128], xp[:, h, :], MTb4[g][:, hh * 128:(hh + 1) * 128], start=(h % 2 == 0), stop=(h % 2 == 1), inc=(h % 2 == 1))
                P.mm(ps[7][:, pr * 128:(pr + 1) * 128], STp[:, h, :], CTb[:, g, csl], start=(h % 2 == 0), stop=(h % 2 == 1), inc=(h % 2 == 1))
            for pr in range(4):
                P.tt("dve", YT, ps[7][:, pr * 128:(pr + 1) * 128], ECU[:, pr, :], ALU.mult)
                P.tt("dve", YT, ps[6][:, pr * 128:(pr + 1) * 128], YT, ALU.add)
                P.stt(YT, XC[:, pr, csl], dsk[:, pr:pr + 1], YT, ALU.mult, ALU.add)
                P.tt("pool", ZT[:, pr, csl], YT, ZT[:, pr, csl], ALU.mult)
            P.tt("dve", Xd, Xtm, DECE.re("p (h o) -> p h o", o=1).bc([128, 8, 64]), ALU.mult)
            for g in range(2):
                P.mm(ps[2][:, g * 256:(g + 1) * 256], Btm[:, g * 128:(g + 1) * 128], Xd[:, 4 * g:4 * g + 4, :].re("p h i -> p (h i)"))
            P.tt("pool", S, S, ECE.re("p (h o) -> p h o", o=1).bc([128, 8, 64]), ALU.mult)
            P.tt("dve", S, S, ps[2].re("p (h i) -> p h i", i=64), ALU.add)
            P.copy("pool", STp[:, 0:8:2, 0:64], S[:, 0:8:2, :])
            P.copy("pool", STp[:, 1:8:2, 64:128], S[:, 1:8:2, :])
        for tb in range(4):
            bs = slice(tb * 512, (tb + 1) * 512)
            for g in range(2):
                for j in range(2):
                    P.tt("pool", SQ[:, j, :], ZT[:, 2 * g + j, bs], ZT[:, 2 * g + j, bs], ALU.mult)
                for j in range(2):
                    P.mm(ps[tb % 2], self.ones, SQ[:, j, :], start=(j == 0), stop=(j == 1), inc=(j == 1))
                P.ts("dve", RS, ps[tb % 2], 1.0 / 256, 1e-5, ALU.mult, ALU.add)
                P.act(RS, RS, AF.Sqrt)
                P.recip(RS, RS)
                for j in range(2):
                    P.stt(YBT[:, b, 2 * g + j, bs], ZT[:, 2 * g + j, bs], ng[:, 2 * g + j:2 * g + j + 1], RS, ALU.mult, ALU.mult)


K.mamba = _mamba


def _bcast_vec(self, name, wdt, n):
    t = self.P.sb(name, [128, n])
    self.P.dma("sp", t, V(wdt.v("c").buf, wdt.ap[0, 0:n].partition_broadcast(128)))
    return t


def _res_ln(self, pss, xin_tile, gam, bet, out_tile, tmp_stats):
    P = self.P
    ALPHA = 4 ** 0.25
    st6, ag = tmp_stats
    for hf in range(2):
        sl = slice(hf * 512, (hf + 1) * 512)
        P.stt(out_tile[:, sl], xin_tile[:, sl], ALPHA, pss[hf], ALU.mult, ALU.add)
        P.op("dve", lambda hf=hf, sl=sl: self.P.nc.vector.bn_stats(out=st6[:, hf, :].ap, in_=out_tile[:, sl].ap), r=(out_tile,), w=(st6,))
    P.op("dve", lambda: self.P.nc.vector.bn_aggr(out=ag.ap, in_=st6.re("p a b -> p (a b)").ap), r=(st6,), w=(ag,))
    P.ts("dve", ag[:, 1:2], ag[:, 1:2], 1e-5, None, ALU.add)
    P.act(ag[:, 1:2], ag[:, 1:2], AF.Sqrt)
    P.recip(ag[:, 1:2], ag[:, 1:2])
    P.ts("dve", out_tile, out_tile, ag[:, 0:1], ag[:, 1:2], ALU.subtract, ALU.mult)
    P.tt("dve", out_tile, out_tile, gam, ALU.mult)
    P.tt("dve", out_tile, out_tile, bet, ALU.add)


def _outproj_ln(self, YM, YBT, wname, gname, bname, Hin, Hout, ya_from_ym_cols=512):
    P = self.P
    sb = P.sb
    ps = self.ps
    nk_tm = 4 if YBT is not None else 8
    gam = self.bcast_vec("op_g", self.W[gname], D)
    bet = self.bcast_vec("op_b", self.W[bname], D)
    wst = sb("op_wst", [128, 8, 512])
    wb = sb("op_wb", [128, 8, D], BF16)
    for hf in range(2):
        self.load_w(self.W[wname], 0, 8, hf * 512, 512, wb[:, :, hf * 512:(hf + 1) * 512], wst)
    yat = [sb(f"op_yat{i}", [128, nk_tm * 128]) for i in range(2)]
    yaT = [sb(f"op_yaT{i}", [128, nk_tm, 128], BF16) for i in range(2)]
    xin = [sb(f"op_xin{i}", [128, D]) for i in range(2)]
    ot = [sb(f"op_ot{i}", [128, D]) for i in range(2)]
    st6 = [sb(f"op_st{i}", [128, 2, 6]) for i in range(2)]
    ag = [sb(f"op_ag{i}", [128, 2]) for i in range(2)]
    n = 0
    for b in range(NB):
        for tt in range(T // 128):
            r0 = b * T + tt * 128
            rs = slice(r0, r0 + 128)
            i2 = n % 2
            P.dma("sp", yat[i2], V(YM.v(("rwo", b, tt)).buf, YM.ap[rs, 0:nk_tm * 128]))
            P.dma("sp", xin[i2], Hin.v(("t", b, tt), (rs, slice(None))))
            for q in range(nk_tm // 4):
                pt = ps[(n * 2 + q) % 2]
                for k in range(4):
                    P.tr(pt[:, k * 128:(k + 1) * 128], yat[i2][:, (q * 4 + k) * 128:(q * 4 + k + 1) * 128], self.ident, inc=(k == 3))
                P.copy("act", yaT[i2][:, q * 4:q * 4 + 4, :], pt.re("p (a c) -> p a c", a=4))
            pss = [ps[2 + (n % 2) * 2], ps[3 + (n % 2) * 2]]
            for hf in range(2):
                for k in range(8):
                    if k < nk_tm:
                        lhs = yaT[i2][:, k, :]
                    else:
                        lhs = YBT[:, b, k - 4, tt * 128:(tt + 1) * 128]
                    P.mm(pss[hf], lhs, wb[:, k, hf * 512:(hf + 1) * 512], start=(k == 0), stop=(k == 7), inc=(k == 7))
            self.res_ln(pss, xin[i2], gam, bet, ot[i2], (st6[i2], ag[i2]))
            P.dma("pool", Hout.v(("t", b, tt), (rs, slice(None))), ot[i2])
            n += 1


def _ffn(self, Hin, Hout, lname, fname):
    P = self.P
    sb = P.sb
    ps = self.ps
    W = self.W
    DFF = 2816
    NJ = DFF // 128
    TB = 1024
    gam = self.bcast_vec("ff_g", W[lname + "_g"], D)
    bet = self.bcast_vec("ff_b", W[lname + "_b"], D)
    cw = []
    for i in range(3):
        t = sb(f"ff_cw{i}", [128, NJ])
        with P.nc.allow_non_contiguous_dma(reason="small"):
            P.dma("sp", t, V(W[fname + "_conv_w"].v("c").buf, W[fname + "_conv_w"].ap[i, :].rearrange("(j p) -> p j", p=128)))
        cw.append(t)
    cb_ = sb("ff_cb", [128, NJ])
    self.col_load("sp", cb_, W[fname + "_conv_b"], slice(0, DFF), NJ, 128)
    hT = sb("ff_hT", [128, 1, 8, 2 + T], BF16)
    wst = [sb(f"ff_wst{i}", [128, 8, 256]) for i in range(2)]
    wd = sb("ff_wd", [128, NJ, D], BF16)
    for j in range(NJ):
        src = W[fname + "_w_down"].v("w", (slice(j * 128, (j + 1) * 128), slice(None)))
        st_ = wst[j % 2].re("p a b -> p (a b)")[:, 0:D]
        P.dma("sp", st_, src)
        P.copy("act" if j % 2 else "dve", wd[:, j, :], st_)
    wgu = [sb(f"ff_wgu{i}", [128, 8, 256], BF16) for i in range(2)]
    G = [sb(f"ff_G{i}", [128, TB + 2]) for i in range(2)]
    ACC = [sb(f"ff_ACC{i}", [128, TB]) for i in range(2)]
    AT = sb("ff_AT", [128, NJ, TB], BF16)
    xin = [sb(f"ff_xin{i}", [128, D]) for i in range(2)]
    ot = [sb(f"ff_ot{i}", [128, D]) for i in range(2)]
    st6 = [sb(f"ff_st{i}", [128, 2, 6]) for i in range(2)]
    ag = [sb(f"ff_ag{i}", [128, 2]) for i in range(2)]
    n = 0
    wup = W[fname + "_w_up"]
    for b in range(NB):
        self.load_hT(Hin, hT, 2, bsel=[b])
        for half in range(T // TB):
            t0 = half * TB
            def fetch(jf):
                if2 = jf % 2
                for q, c0 in enumerate((jf * 128, DFF + jf * 128)):
                    src = wup.v("w", (slice(None), slice(c0, c0 + 128))).re("(k p) c -> p k c", p=128)
                    P.dma("sp", wst[if2][:, :, q * 128:(q + 1) * 128], src)
                P.copy("act", wgu[if2], wst[if2])
            fetch(0)
            for j in range(NJ):
                i2 = j % 2
                wg = wgu[i2]
                if j + 1 < NJ:
                    fetch(j + 1)
                g_ = G[i2]
                for blk, (o0, on) in enumerate(((0, 512), (512, 512), (1024, 2))):
                    pg = ps[4 + blk % 2] if blk < 2 else ps[6]
                    for k in range(8):
                        P.mm(pg[:, 0:on], wg[:, k, 0:128], hT[:, 0, k, t0 + o0: t0 + o0 + on], start=(k == 0), stop=(k == 7), inc=(k == 7))
                    P.copy("act", g_[:, o0:o0 + on], pg[:, 0:on])
                a_ = ACC[i2]
                P.ts("dve", a_, g_[:, 0:TB], cw[0][:, j:j + 1], cb_[:, j:j + 1], ALU.mult, ALU.add)
                P.stt(a_, g_[:, 1:TB + 1], cw[1][:, j:j + 1], a_, ALU.mult, ALU.add)
                P.stt(a_, g_[:, 2:TB + 2], cw[2][:, j:j + 1], a_, ALU.mult, ALU.add)
                P.act(a_, a_, AF.Silu)
                for blk in range(2):
                    pu = ps[(j % 2) * 2 + blk]
                    for k in range(8):
                        P.mm(pu, wg[:, k, 128:256], hT[:, 0, k, 2 + t0 + blk * 512: 2 + t0 + (blk + 1) * 512], start=(k == 0), stop=(k == 7), inc=(k == 7))
                    P.tt("dve", AT[:, j, blk * 512:(blk + 1) * 512], a_[:, blk * 512:(blk + 1) * 512], pu, ALU.mult)
            for t8 in range(TB // 128):
                tt = (t0 // 128) + t8
                r0 = b * T + tt * 128
                rs = slice(r0, r0 + 128)
                i2 = n % 2
                P.dma("sp", xin[i2], Hin.v(("t", b, tt), (rs, slice(None))))
                pss = [ps[(n % 2) * 2], ps[(n % 2) * 2 + 1]]
                for hf in range(2):
                    for j in range(NJ):
                        P.mm(pss[hf], AT[:, j, t8 * 128:(t8 + 1) * 128], wd[:, j, hf * 512:(hf + 1) * 512], start=(j == 0), stop=(j == NJ - 1), inc=(j == NJ - 1))
                self.res_ln(pss, xin[i2], gam, bet, ot[i2], (st6[i2], ag[i2]))
                P.dma("pool", Hout.v(("t", b, tt), (rs, slice(None))), ot[i2])
                n += 1


K.bcast_vec = _bcast_vec
K.res_ln = _res_ln
K.outproj_ln = _outproj_ln
K.ffn = _ffn


def _copy_out(self, src, dst):
    P = self.P
    tl = [P.sb(f"co_t{i}", [128, D]) for i in range(2)]
    n = 0
    for b in range(NB):
        for tt in range(T // 128):
            r0 = b * T + tt * 128
            rs = slice(r0, r0 + 128)
            P.dma("sp", tl[n % 2], src.v(("t", b, tt), (rs, slice(None))))
            P.dma("pool", dst.v(("t", b, tt), (rs, slice(None))), tl[n % 2])
            n += 1


K.copy_out = _copy_out


def build(stop_after=None, dbg=()):
    k = K(stop_after=stop_after, dbg=dbg)
    P = k.P
    k.layer0_mixer()
    if stop_after in ("inproj0", "rwkv", "mamba", "h1", "h2"):
        k.P.finish()
        return k
    k.layer1_mixer(k.H2)
    if stop_after in ("inproj1", "sb", "mla"):
        k.P.finish()
        return k
    k.H3 = k.scratch("H3", [NT, D])
    with P.phase():
        k.outproj_ln(k.YM, None, "l1_w_out", "l1_ln1_g", "l1_ln1_b", k.H2, k.H3)
    if stop_after == "h3":
        k.P.finish()
        return k
    with P.phase():
        k.ffn(k.H3, k.out, "l1_ln2", "ffn1")
    k.P.finish()
    return k


def kernel(**inputs):
    k = build()
    in_maps = [_in_map(inputs, c) for c in range(8)]
    res = run_bass_kernel_spmd(k.P.nc, in_maps, core_ids=list(range(8)))
    outs = [np.asarray(r["out"]).reshape(NB, T, D) for r in res.results]
    return np.concatenate(outs, axis=0).astype(np.float32)


def _l1_inproj(self, Hin, PT1, VSB):
    P = self.P
    with P.phase():
        hT = P.sb("hT1", [128, NB, 8, T], BF16)
        self.load_hT(Hin, hT, 0)
        tiles = [(i * 128, 128) for i in range(8)] + [(i * 128, 128) for i in range(12, 15)] + [(1920, 32)]
        self.inproj(hT, 0, self.W["l1_w_in"], tiles, PT1)
        wst = P.sb("ip1_wst", [128, 8, 32]); wb = P.sb("ip1_wb", [128, 8, 32], BF16)
        wdt = self.W["l1_w_in"]
        for q, c0 in enumerate((1936, 1920)):
            src = wdt.v("w", (slice(None), slice(c0, c0 + 16))).re("(k p) c -> p k c", p=128)
            with P.nc.allow_non_contiguous_dma(reason="small"):
                P.dma("sp", wst[:, :, q * 16:(q + 1) * 16], src)
        P.copy("pool", wb, wst)
        o = P.sb("ip1_o", [32, T])
        for b in range(NB):
            for tb in range(4):
                ps = self.ps[tb % 2]
                for k in range(8):
                    P.mm(ps[0:32, :], wb[:, k, :], hT[:, b, k, tb * 512:(tb + 1) * 512], start=(k == 0), stop=(k == 7), inc=(k == 7))
                P.copy("act", o[:, tb * 512:(tb + 1) * 512], ps[0:32, :])
            P.dma("pool", PT1.v(("sw", b), (slice(1952, 1984), slice(b * T, (b + 1) * T))), o)
        wst2 = P.sb("ip1_wst2", [128, 8, 512]); wv = P.sb("ip1_wv", [128, 8, 512], BF16)
        self.load_w(wdt, 0, 8, 1024, 512, wv, wst2)
        vt = [P.sb(f"ip1_vt{i}", [128, 512], BF16) for i in range(2)]
        n = 0
        for b in range(NB):
            for tt in range(T // 128):
                ps = self.ps[2 + n % 2]
                for k in range(8):
                    P.mm(ps, hT[:, b, k, tt * 128:(tt + 1) * 128], wv[:, k, :], start=(k == 0), stop=(k == 7), inc=(k == 7))
                P.copy("act" if n % 2 else "dve", vt[n % 2], ps)
                r0 = b * T + tt * 128
                P.dma("pool", VSB.v(("t", b, tt), (slice(r0, r0 + 128), slice(None))), vt[n % 2])
                n += 1


def _attn_consts(self, mode):
    P = self.P
    o5 = P.sb("at_o5", [128, 512])
    P.memset("pool", o5, 1.0)
    self.m_incl = []
    self.m_strict = []
    for j in range(4):
        if mode == "sm":
            mi = P.sb(f"at_mi{j}", [128, 512], BF16)
            P.affsel(mi, o5, [[1, 512]], ALU.is_ge, 0.0, -128 * j, -1)
            self.m_incl.append(mi)
        else:
            ms = P.sb(f"at_ms{j}", [128, 512])
            P.affsel(ms, o5, [[1, 512]], ALU.is_gt, 0.0, -128 * j, -1)
            self.m_strict.append(ms)
    if mode == "sb":
        self.tris = P.sb("at_tris", [128, 128])
        P.affsel(self.tris, self.ones, [[-1, 128]], ALU.is_gt, 0.0, 0, 1)


def _attn(self, mode, QT, KT, kd, Vt, vw, YM, b, col0, scale, nh=1):
    P = self.P
    ps = self.ps
    sb = P.sb
    tg = mode
    if getattr(self, "_at_ws_phase", None) is not P.phase_es:
        self._at_ws_phase = P.phase_es
        wsl = self._at_ws = []
        for u in range(nh):
            ws = {}
            ws["attT"] = sb(f"at_attT{tg}{u}", [128, 16, 512], BF16)
            ws["yo"] = [sb(f"at_yo{tg}{u}{i}", [128, 4, 64]) for i in range(2)]
            ws["rs"] = sb(f"at_rs{tg}{u}", [128, 4])
            if mode == "sb":
                ws["E1"] = [sb(f"at_E1{tg}{u}{i}", [128, 512]) for i in range(2)]
                ws["LK"] = [sb(f"at_LK{tg}{u}{i}", [128, 512]) for i in range(2)]
                ws["T1"] = [sb(f"at_T1{tg}{u}{i}", [128, 512]) for i in range(2)]
                ws["ACC"] = sb(f"at_ACC{tg}{u}", [128, 512])
            wsl.append(ws)
    wsl = self._at_ws
    cnt = 0
    for hg in range(8 // nh):
        for qb in range(4):
            qs_ = slice(qb * 512, (qb + 1) * 512)
            nkb = 4 * qb + 4
            kbs = list(range(nkb - 1, -1, -1))

            def s12(it, u):
                kb = kbs[it]
                j = kb - 4 * qb
                h = hg * nh + u
                ws = wsl[u]
                attT = ws["attT"]
                pz = ps[u] if nh > 1 else ps[it % 2]
                P.mm(pz, KT[0:kd, h, kb * 128:(kb + 1) * 128], QT[0:kd, h, qs_])
                if mode == "sm":
                    P.act(attT[:, kb, :], pz, AF.Exp, scale=scale)
                    if j >= 0:
                        P.tt("pool", attT[:, kb, :], attT[:, kb, :], self.m_incl[j], ALU.mult)
                else:
                    i2 = it % 2
                    E1, LK, T1 = ws["E1"][i2], ws["LK"][i2], ws["T1"][i2]
                    P.act(E1, pz, AF.Exp, scale=scale)
                    P.act(LK, E1, AF.Ln, bias=1.0)
                    P.stt(T1, pz, scale, LK, ALU.mult, ALU.subtract)
                    if j >= 0:
                        P.tt("pool", LK, LK, self.m_strict[j], ALU.mult)

            def s34(it, u):
                kb = kbs[it]
                j = kb - 4 * qb
                ws = wsl[u]
                attT = ws["attT"]
                i2 = it % 2
                LK, T1, ACC = ws["LK"][i2], ws["T1"][i2], ws["ACC"]
                pc = ps[2 + u] if nh > 1 else ps[2 + it % 2]
                P.mm(pc, self.tris, LK, start=True, stop=(it == 0), inc=(it == 0))
                if it > 0:
                    P.mm(pc, self.ones, ACC, start=False, stop=True)
                P.tt("dve", T1, T1, pc, ALU.subtract)
                P.act(attT[:, kb, :], T1, AF.Exp)
                if j >= 0:
                    P.tt("pool", attT[:, kb, :], attT[:, kb, :], self.m_strict[j], ALU.mult)
                if it == 0:
                    P.copy("pool", ACC, LK)
                elif kb > 0:
                    P.tt("dve", ACC, ACC, LK, ALU.add)

            for it in range(nkb + 1):
                if it < nkb:
                    for u in range(nh):
                        s12(it, u)
                if mode == "sb" and it >= 1:
                    for u in range(nh):
                        s34(it - 1, u)
            for u in range(nh):
                h = hg * nh + u
                ws = wsl[u]
                attT = ws["attT"]
                py = ps[4 + cnt % 2]
                y_ = ws["yo"][qb % 2]
                rs_ = ws["rs"]
                cnt += 1
                for qs in range(4):
                    last = 4 * qb + qs
                    for kb in range(last + 1):
                        P.mm(py[:, qs * vw:(qs + 1) * vw], attT[:, kb, qs * 128:(qs + 1) * 128], Vt[:, kb, h, :],
                             start=(kb == 0), stop=(kb == last), inc=(kb == last))
                pv = py[:, 0:4 * vw].re("p (a c) -> p a c", c=vw)
                if mode == "sm":
                    P.copy("act", rs_, pv[:, :, 64])
                    P.recip(rs_, rs_)
                    P.tt("dve", y_, pv[:, :, 0:64], rs_.re("p (a o) -> p a o", o=1).bc([128, 4, 64]), ALU.mult)
                else:
                    P.copy("act", y_, pv)
                r0 = b * T + qb * 512
                dst = V(YM.v(("at", mode, b, h, qb)).buf, YM.ap[r0:r0 + 512, col0 + h * 64: col0 + (h + 1) * 64].rearrange("(a p) c -> p a c", p=128))
                P.dma("pool", dst, y_)


K.l1_inproj = _l1_inproj
K.attn_consts = _attn_consts
K.attn = _attn


def _layer1_mixer(self, Hin):
    P = self.P
    W = self.W
    sb = P.sb
    ps = self.ps
    PT1 = self.PT1 = self.scratch("PT1", [1984, NT])
    VSB = self.VSB = self.scratch("VSB", [NT, 512], BF16)
    YM = self.YM
    self.l1_inproj(Hin, PT1, VSB)
    if self.stop_after == "inproj1":
        return
    with P.phase():
        self.attn_consts("sb")
        QT = sb("sb_QT", [64, 8, T], BF16); KT = sb("sb_KT", [64, 8, T], BF16)
        Vt = sb("sb_Vt", [128, 16, 8, 64], BF16)
        stg = [sb(f"sb_stg{i}", [64, T]) for i in range(2)]
        n = 0
        for b in range(NB):
            tsl = slice(b * T, (b + 1) * T)
            for h in range(8):
                for q, (dst, r0) in enumerate(((QT, h * 64), (KT, 512 + h * 64))):
                    s_ = stg[n % 2]
                    P.dma("sp", s_, PT1.v(("c", r0 // 128, b), (slice(r0, r0 + 64), tsl)))
                    P.copy("dve" if n % 2 else "act", dst[:, h, :], s_)
                    n += 1
            P.barrier()
            P.dma("sp", Vt.re("p a h c -> p a (h c)"), V(VSB.v(("t", b, 0)).buf, VSB.ap[tsl, :].rearrange("(a p) c -> p a c", p=128)))
            self.attn("sb", QT, KT, 64, Vt, 64, YM, b, 0, 64 ** -0.5, nh=2)
    if self.stop_after == "sb":
        return
    with P.phase():
        self.attn_consts("sm")
        rc = sb("ml_rc", [96, 2])
        P.dma("sp", rc[64:96, :], self.rope_c.v("c"))
        qg = sb("ml_qg", [128, 2]); self.col_load("sp", qg, W["mla_q_norm_g"], slice(0, 256), 2, 128)
        kvg = sb("ml_kvg", [128, 1]); self.col_load("sp", kvg, W["mla_kv_norm_g"], slice(0, 128), 1, 128)
        CQ = sb("ml_CQ", [128, 2, T]); CKV = sb("ml_CKV", [128, T])
        wst = CQ[:, :, 0:768]
        wuq = sb("ml_wuq", [128, 2, 768], BF16)
        P.dma("sp", wst, W["mla_w_uq"].v("w").re("(k p) c -> p k c", p=128))
        P.copy("pool", wuq, wst)
        wsw = sb("ml_wsw", [128, 2, 8, 96], BF16)
        P.memset("pool", wsw, 0.0)
        w4 = wst.re("p k (h c) -> p k h c", c=96)
        P.copy("pool", wsw[:, :, :, 64:80], w4[:, :, :, 80:96])
        P.copy("pool", wsw[:, :, :, 80:96], w4[:, :, :, 64:80])
        wst2 = CKV[:, 0:1024]
        P.dma("sp", wst2, W["mla_w_ukv"].v("w"))
        wuk = sb("ml_wuk", [128, 8, 64], BF16); wv = sb("ml_wv", [128, 8, 64], BF16)
        w5 = wst2.re("p (h c) -> p h c", c=128)
        P.copy("pool", wuk, w5[:, :, 0:64])
        P.copy("pool", wv, w5[:, :, 64:128])
        QT = sb("ml_QT", [96, 8, T], BF16); KT = sb("ml_KT", [96, 8, T], BF16)
        Vt = sb("ml_Vt", [128, 16, 8, 65], BF16)
        P.memset("pool", Vt, 1.0)
        CQN = sb("ml_CQN", [128, 2, T], BF16); CKVN = sb("ml_CKVN", [128, T], BF16)
        SQ = sb("ml_SQ", [128, 2, 512]); RS = sb("ml_RS", [128, 512])
        ANG = sb("ml_ANG", [96, T]); COS2 = sb("ml_COS", [96, T]); SINS = sb("ml_SIN", [96, T])
        KPE = sb("ml_KPE", [96, T]); KSW = sb("ml_KSW", [96, T])
        KQ = KSW
        POSI = V(KPE.buf, KPE.ap.bitcast(I32))
        TA = [sb(f"ml_TA{i}", [96, 512]) for i in range(2)]; TB_ = [sb(f"ml_TB{i}", [96, 512]) for i in range(2)]
        R = slice(64, 96)
        TWO_PI = 2.0 * math.pi
        C1 = 6.28125
        C2 = float(np.float32(TWO_PI - C1))
        C3 = float(TWO_PI - C1 - C2)
        MAGIC = 12582912.0
        for b in range(NB):
            tsl = slice(b * T, (b + 1) * T)
            for j in range(2):
                P.dma("sp", CQ[:, j, :], PT1.v(("c", 12 + j, b), (slice(1536 + j * 128, 1536 + (j + 1) * 128), tsl)))
            P.dma("sp", CKV, PT1.v(("c", 14, b), (slice(1792, 1920), tsl)))
            P.dma("sp", POSI[R, :], V(self.pos.v("c").buf, self.pos.ap[b, :].partition_broadcast(32)))
            for tb in range(4):
                bs = slice(tb * 512, (tb + 1) * 512)
                for j in range(2):
                    P.tt("pool", SQ[:, j, :], CQ[:, j, bs], CQ[:, j, bs], ALU.mult)
                for j in range(2):
                    P.mm(ps[tb % 2], self.ones, SQ[:, j, :], start=(j == 0), stop=(j == 1), inc=(j == 1))
                P.ts("dve", RS, ps[tb % 2], 1.0 / 256, 1e-6, ALU.mult, ALU.add)
                P.act(RS, RS, AF.Sqrt)
                P.recip(RS, RS)
                for j in range(2):
                    P.stt(CQN[:, j, bs], CQ[:, j, bs], qg[:, j:j + 1], RS, ALU.mult, ALU.mult)
                P.tt("pool", SQ[:, 0, :], CKV[:, bs], CKV[:, bs], ALU.mult)
                P.mm(ps[2 + tb % 2], self.ones, SQ[:, 0, :])
                P.ts("dve", RS, ps[2 + tb % 2], 1.0 / 128, 1e-6, ALU.mult, ALU.add)
                P.act(RS, RS, AF.Sqrt)
                P.recip(RS, RS)
                P.stt(CKVN[:, bs], CKV[:, bs], kvg[:, 0:1], RS, ALU.mult, ALU.mult)
            P.copy("dve", ANG[R, :], POSI[R, :])
            P.ts("dve", ANG[R, :], ANG[R, :], rc[R, 0:1], None, ALU.mult)
            P.ts("dve", KQ[R, :], ANG[R, :], 1.0 / TWO_PI, MAGIC, ALU.mult, ALU.add)
            P.ts("dve", KQ[R, :], KQ[R, :], MAGIC, None, ALU.subtract)
            P.stt(ANG[R, :], KQ[R, :], -C1, ANG[R, :], ALU.mult, ALU.add)
            P.stt(ANG[R, :], KQ[R, :], -C2, ANG[R, :], ALU.mult, ALU.add)
            P.stt(ANG[R, :], KQ[R, :], -C3, ANG[R, :], ALU.mult, ALU.add)
            P.ts("dve", ANG[R, :], ANG[R, :], math.pi, -math.pi, ALU.min, ALU.max)
            P.act(SINS[R, :], ANG[R, :], AF.Sin)
            P.ts("dve", SINS[R, :], SINS[R, :], rc[R, 1:2], None, ALU.mult)
            P.ts("dve", KQ[R, :], ANG[R, :], math.pi / 2, None, ALU.is_gt)
            P.stt(ANG[R, :], KQ[R, :], -TWO_PI, ANG[R, :], ALU.mult, ALU.add)
            P.ts("dve", ANG[R, :], ANG[R, :], math.pi / 2, math.pi, ALU.add, ALU.min)
            P.act(COS2[R, :], ANG[R, :], AF.Sin)
            P.dma("sp", KPE[R, :], PT1.v(("c", 15, b), (slice(1920, 1952), tsl)))
            P.dma("sp", KSW[R, :], PT1.v(("sw", b), (slice(1952, 1984), tsl)))
            P.tt("dve", KPE[R, :], KPE[R, :], COS2[R, :], ALU.mult)
            P.tt("dve", KSW[R, :], KSW[R, :], SINS[R, :], ALU.mult)
            P.tt("dve", KPE[R, :], KPE[R, :], KSW[R, :], ALU.add)
            for h in range(8):
                P.copy("pool" if h % 2 else "act", KT[R, h, :], KPE[R, :])
            n = 0
            for h in range(8):
                for tb in range(4):
                    bs = slice(tb * 512, (tb + 1) * 512)
                    pq = ps[n % 2]; pw = ps[2 + n % 2]; pk = ps[4 + n % 2]
                    for k in range(2):
                        P.mm(pq[0:96, :], wuq[:, k, h * 96:(h + 1) * 96], CQN[:, k, bs], start=(k == 0), stop=(k == 1), inc=(k == 1))
                    for k in range(2):
                        P.mm(pw[0:96, :], wsw[:, k, h, :], CQN[:, k, bs], start=(k == 0), stop=(k == 1), inc=(k == 1))
                    P.mm(pk[0:64, :], wuk[:, h, :], CKVN[:, bs])
                    P.copy("act", QT[0:64, h, bs], pq[0:64, :])
                    ta = TA[n % 2]; tb_ = TB_[n % 2]
                    P.tt("dve", ta[R, :], pq[R, :], COS2[R, bs], ALU.mult)
                    P.tt("dve", tb_[R, :], pw[R, :], SINS[R, bs], ALU.mult)
                    P.tt("pool", QT[R, h, bs], ta[R, :], tb_[R, :], ALU.add)
                    P.copy("act", KT[0:64, h, bs], pk[0:64, :])
                    n += 1
            for tt in range(T // 128):
                pv = ps[6 + tt % 2]
                P.mm(pv, CKVN[:, tt * 128:(tt + 1) * 128], wv.re("p h c -> p (h c)"))
                P.copy("act" if tt % 2 else "dve", Vt[:, tt, :, 0:64], pv.re("p (h c) -> p h c", c=64))
            self.attn("sm", QT, KT, 96, Vt, 65, YM, b, 512, 96 ** -0.5)


K.layer1_mixer = _layer1_mixer
```

```python
import contextlib
import math
import numpy as np
import concourse.bass as bass
import concourse.mybir as mybir
from concourse.bass_utils import run_bass_kernel_spmd

F32 = mybir.dt.float32
BF16 = mybir.dt.bfloat16
I32 = mybir.dt.int32
AF = mybir.ActivationFunctionType
ALU = mybir.AluOpType
AX = mybir.AxisListType

T = 2048
NB = 2
NT = NB * T
D = 1024
NDS = 8


class Buf:
    __slots__ = ("name", "w", "r")

    def __init__(self, name):
        self.name = name
        self.w = None
        self.r = {}


class V:
    __slots__ = ("buf", "ap")

    def __init__(self, buf, ap):
        self.buf = buf
        self.ap = ap

    def __getitem__(self, idx):
        return V(self.buf, self.ap[idx])

    def re(self, s, **kw):
        return V(self.buf, self.ap.rearrange(s, **kw))

    def bc(self, shape):
        return V(self.buf, self.ap.broadcast_to(shape))


class DramT:
    def __init__(self, ap, name):
        self.ap = ap
        self.name = name
        self.bufs = {}

    def v(self, key, idx=None):
        b = self.bufs.get(key)
        if b is None:
            b = self.bufs[key] = Buf(f"{self.name}:{key}")
        ap = self.ap if idx is None else self.ap[idx]
        return V(b, ap)


class Prog:
    def __init__(self):
        self.nc = nc = bass.Bass("TRN2", target_bir_lowering=False)
        self.es = contextlib.ExitStack()
        self.eng = {"pe": nc.tensor, "act": nc.scalar, "dve": nc.vector, "pool": nc.gpsimd, "sp": nc.sync}
        self.semh = {}
        self.cnt = {}
        for e in ("pe", "act", "dve", "pool"):
            self.semh[e] = self.es.enter_context(nc.semaphore("sem_" + e))
            self.cnt[e] = 0
        self.seen = {e: {} for e in self.eng}
        self.dq = {}
        for q in ("sp", "pool", "act"):
            names = []
            for i in range(NDS):
                n = f"d_{q}_{i}"
                self.semh[n] = self.es.enter_context(nc.semaphore(n))
                names.append(n)
            self.dq[q] = dict(names=names, vals=[0] * NDS, i=0)
        self.nps = 0
        self.phase_es = None

    def sb(self, name, shape, dtype=F32):
        es = self.phase_es if self.phase_es is not None else self.es
        self.nps += 1
        name = f"{name}_u{self.nps}"
        h = es.enter_context(self.nc.sbuf_tensor(name, list(shape), dtype))
        return V(Buf(name), h[:])

    @contextlib.contextmanager
    def phase(self):
        self.phase_es = contextlib.ExitStack()
        try:
            yield
        finally:
            self.barrier()
            self.phase_es.close()
            self.phase_es = None

    def barrier(self):
        toks = []
        for q, dq in self.dq.items():
            for n, v in zip(dq["names"], dq["vals"]):
                if v > 0:
                    toks.append((n, v))
        for e in ("pe", "act", "dve", "pool"):
            if self.cnt[e] > 0:
                toks.append((e, self.cnt[e]))
        for e in self.eng:
            for n, v in toks:
                if n != e:
                    self._wait(e, n, v)

    def ps(self, name, shape=(128, 512), dtype=F32):
        h = self.es.enter_context(self.nc.psum_tensor(name, list(shape), dtype))
        return V(Buf(name), h[:])

    def dram(self, name, shape, dtype=F32, kind="Internal"):
        ap = self.nc.dram_tensor(name, list(shape), dtype, kind=kind).ap()
        return DramT(ap, name)

    def _wait(self, e, s, v):
        if self.seen[e].get(s, 0) >= v:
            return
        self.seen[e][s] = v
        self.eng[e].wait_ge(self.semh[s], v)

    def _deps(self, e, reads, writes):
        raw = {}
        oth = {}

        def add(d, tok):
            if tok is None:
                return
            s, v = tok
            if d.get(s, 0) < v:
                d[s] = v

        for b in reads:
            add(raw, b.w)
        for b in writes:
            add(oth, b.w)
            for s, v in b.r.items():
                add(oth, (s, v))
        for s, v in raw.items():
            self._wait(e, s, v)
        for s, v in oth.items():
            if s == e and e == "pe":
                continue
            self._wait(e, s, v)

    def op(self, e, fn, r=(), w=(), inc=True):
        rb = [x.buf for x in r if isinstance(x, V)]
        wb = [x.buf for x in w]
        self._deps(e, rb, wb)
        ins = fn()
        if inc:
            self.cnt[e] += 1
            ins.then_inc(self.semh[e], 1)
            val = self.cnt[e]
        else:
            val = self.cnt[e] + 1
        for b in rb:
            if b.r.get(e, 0) < val:
                b.r[e] = val
        for b in wb:
            b.w = (e, val)
            b.r = {}
        return ins

    def dma(self, q, out, in_):
        dq = self.dq[q]
        i = dq["i"]
        dq["i"] = (i + 1) % NDS
        n = dq["names"][i]
        if dq["vals"][i] > 0:
            self._wait(q, n, dq["vals"][i])
        self._deps(q, [in_.buf], [out.buf])
        dq["vals"][i] += 16
        val = dq["vals"][i]
        self.eng[q].dma_start(out=out.ap, in_=in_.ap).then_inc(self.semh[n], 16)
        if in_.buf.r.get(n, 0) < val:
            in_.buf.r[n] = val
        out.buf.w = (n, val)
        out.buf.r = {}

    def finish(self):
        for q, dq in self.dq.items():
            for n, v in zip(dq["names"], dq["vals"]):
                if v > 0:
                    self._wait("sp", n, v)
        for e in ("pe", "act", "dve", "pool"):
            if self.cnt[e] > 0:
                self._wait("sp", e, self.cnt[e])

    def _a(self, x):
        return x.ap if isinstance(x, V) else x

    def tt(self, e, out, in0, in1, op):
        return self.op(e, lambda: self.eng[e].tensor_tensor(out=out.ap, in0=in0.ap, in1=in1.ap, op=op), r=(in0, in1), w=(out,))

    def ts(self, e, out, in0, s1, s2, op0, op1=None):
        if op1 is None:
            return self.op(e, lambda: self.eng[e].tensor_scalar(out=out.ap, in0=in0.ap, scalar1=self._a(s1), scalar2=None, op0=op0), r=(in0, s1), w=(out,))
        return self.op(e, lambda: self.eng[e].tensor_scalar(out=out.ap, in0=in0.ap, scalar1=self._a(s1), scalar2=self._a(s2), op0=op0, op1=op1), r=(in0, s1, s2), w=(out,))

    def stt(self, out, in0, s, in1, op0, op1):
        return self.op("dve", lambda: self.nc.vector.scalar_tensor_tensor(out=out.ap, in0=in0.ap, scalar=self._a(s), in1=in1.ap, op0=op0, op1=op1), r=(in0, s, in1), w=(out,))

    def act(self, out, in_, func, bias=0.0, scale=1.0):
        return self.op("act", lambda: self.nc.scalar.activation(out=out.ap, in_=in_.ap, func=func, bias=self._a(bias), scale=self._a(scale)), r=(in_, bias, scale), w=(out,))

    def copy(self, e, out, in_):
        if e == "act":
            return self.op(e, lambda: self.nc.scalar.copy(out=out.ap, in_=in_.ap), r=(in_,), w=(out,))
        return self.op(e, lambda: self.eng[e].tensor_copy(out=out.ap, in_=in_.ap), r=(in_,), w=(out,))

    def memset(self, e, out, val):
        return self.op(e, lambda: self.eng[e].memset(out.ap, val), w=(out,))

    def mm(self, out, lhsT, rhs, start=True, stop=True, inc=True):
        return self.op("pe", lambda: self.nc.tensor.matmul(out.ap, lhsT.ap, rhs.ap, start=start, stop=stop), r=(lhsT, rhs), w=(out,), inc=inc)

    def tr(self, out, in_, ident, inc=True):
        return self.op("pe", lambda: self.nc.tensor.transpose(out.ap, in_.ap, ident.ap), r=(in_, ident), w=(out,), inc=inc)

    def recip(self, out, in_):
        return self.op("dve", lambda: self.nc.vector.reciprocal(out=out.ap, in_=in_.ap), r=(in_,), w=(out,))

    def reduce(self, out, in_, op=ALU.add, axis=AX.X):
        return self.op("dve", lambda: self.nc.vector.tensor_reduce(out=out.ap, in_=in_.ap, axis=axis, op=op), r=(in_,), w=(out,))

    def scan(self, out, d0, d1, init, op0, op1):
        return self.op("dve", lambda: self.nc.vector.tensor_tensor_scan(out=out.ap, data0=d0.ap, data1=d1.ap, initial=self._a(init), op0=op0, op1=op1), r=(d0, d1, init), w=(out,))

    def affsel(self, out, in_, pattern, cmp, fill, base, cm):
        return self.op("pool", lambda: self.nc.gpsimd.affine_select(out=out.ap, in_=in_.ap, pattern=pattern, compare_op=cmp, fill=fill, base=base, channel_multiplier=cm), r=(in_,), w=(out,))


WSPEC = {
    "l0_w_in": (1024, 3336), "rwkv_mix": (1792,), "rwkv_w0": (512,), "rwkv_w2": (64, 512), "rwkv_a0": (512,),
    "rwkv_a2": (64, 512), "rwkv_g2": (128, 512), "rwkv_k_k": (512,), "rwkv_k_a": (512,), "rwkv_r_k": (8, 64),
    "rwkv_ln_g": (512,), "rwkv_ln_b": (512,), "ssm_conv_w": (4, 1024), "ssm_conv_b": (1024,), "ssm_dt_bias": (8,),
    "ssm_a_log": (8,), "ssm_d": (8,), "ssm_norm_g": (512,), "l0_w_out": (1024, 1024), "l0_ln1_g": (1024,),
    "l0_ln1_b": (1024,), "ffn0_w_up": (1024, 5632), "ffn0_conv_w": (3, 2816), "ffn0_conv_b": (2816,),
    "ffn0_w_down": (2816, 1024), "l0_ln2_g": (1024,), "l0_ln2_b": (1024,), "l1_w_in": (1024, 1952),
    "mla_q_norm_g": (256,), "mla_w_uq": (256, 768), "mla_kv_norm_g": (128,), "mla_w_ukv": (128, 1024),
    "l1_w_out": (1024, 1024), "l1_ln1_g": (1024,), "l1_ln1_b": (1024,), "ffn1_w_up": (1024, 5632),
    "ffn1_conv_w": (3, 2816), "ffn1_conv_b": (2816,), "ffn1_w_down": (2816, 1024), "l1_ln2_g": (1024,),
    "l1_ln2_b": (1024,),
}


class K:
    def __init__(self, stop_after=None, dbg=()):
        self.P = P = Prog()
        self.stop_after = stop_after
        self.dbg = set(dbg)
        self.W = {}
        for n, shp in WSPEC.items():
            s2 = list(shp) if len(shp) == 2 else [1, shp[0]]
            self.W[n] = P.dram(n, s2, F32, kind="ExternalInput")
        self.x = P.dram("x", [NT, D], F32, kind="ExternalInput")
        self.pos = P.dram("positions", [NB, T], I32, kind="ExternalInput")
        self.out = P.dram("out", [NT, D], F32, kind="ExternalOutput")
        self.rope_c = P.dram("rope_c", [32, 2], F32, kind="ExternalInput")
        self.ps = [P.ps(f"psb{i}") for i in range(8)]
        self.consts()

    def scratch(self, name, shape, dtype=F32):
        kind = "ExternalOutput" if name in self.dbg else "Internal"
        return self.P.dram(name, shape, dtype, kind=kind)

    def consts(self):
        P = self.P
        self.ones = P.sb("ones", [128, 128])
        P.memset("pool", self.ones, 1.0)
        self.ident = P.sb("ident", [128, 128])
        P.affsel(self.ident, self.ones, [[-1, 128]], ALU.is_equal, 0.0, 0, 1)
        self.identb = P.sb("identb", [128, 128], BF16)
        P.copy("pool", self.identb, self.ident)
        self.onesb = P.sb("onesb", [128, 128], BF16)
        P.copy("pool", self.onesb, self.ones)

    def col_load(self, q, dst, src_dt, idx, n, p):
        P = self.P
        src = V(src_dt.v("c").buf, src_dt.ap[0, idx].rearrange("(j p) -> p j", p=p))
        with P.nc.allow_non_contiguous_dma(reason="small per-channel vector"):
            P.dma(q, dst, src)

    def load_hT(self, src, hT, halo, bsel=None):
        P = self.P
        if not hasattr(self, "_ld_xt") or self._ld_xt_phase is not P.phase_es:
            self._ld_xt = [P.sb(f"ld_xt{i}", [128, D]) for i in range(2)]
            self._ld_xt_phase = P.phase_es
        xt = self._ld_xt
        n = 0
        for b_src in (range(NB) if bsel is None else bsel):
            b = b_src if bsel is None else 0
            if halo:
                P.memset("pool", hT[:, b, :, 0:halo], 0.0)
            for tt in range(T // 128):
                x_ = xt[n % 2]
                r0 = b_src * T + tt * 128
                P.dma("sp", x_, src.v(("t", b_src, tt), (slice(r0, r0 + 128), slice(None))))
                for half in range(2):
                    ps = self.ps[(2 * n + half) % 4]
                    for kk in range(4):
                        kc = half * 4 + kk
                        P.tr(ps[:, kk * 128:(kk + 1) * 128], x_[:, kc * 128:(kc + 1) * 128], self.ident, inc=(kk == 3))
                    eng = "act" if half == 0 else "dve"
                    P.copy(eng, hT[:, b, half * 4:half * 4 + 4, halo + tt * 128: halo + (tt + 1) * 128],
                           ps.re("p (a c) -> p a c", a=4))
                n += 1

    def load_w(self, wdt, k0, kn, c0, cn, dst, stage):
        P = self.P
        src = wdt.v("w", (slice(k0 * 128, (k0 + kn) * 128), slice(c0, c0 + cn))).re("(k p) c -> p k c", p=128)
        P.dma("sp", stage[:, 0:kn, 0:cn], src)
        P.copy("act", dst, stage[:, 0:kn, 0:cn])

    def inproj(self, hT, halo, wdt, col_tiles, PT):
        P = self.P
        wst = [P.sb(f"ip_wst{i}", [128, 8, 128]) for i in range(2)]
        wb = [P.sb(f"ip_wb{i}", [128, 8, 128], BF16) for i in range(2)]
        ost = [P.sb(f"ip_o{i}", [128, T]) for i in range(2)]
        n = 0
        c0_, cn_ = col_tiles[0]
        self.load_w(wdt, 0, 8, c0_, cn_, wb[0][:, :, 0:cn_], wst[0])
        for ci, (c0, cn) in enumerate(col_tiles):
            w_ = wb[ci % 2]
            if ci + 1 < len(col_tiles):
                c0_, cn_ = col_tiles[ci + 1]
                self.load_w(wdt, 0, 8, c0_, cn_, wb[(ci + 1) % 2][:, :, 0:cn_], wst[(ci + 1) % 2])
            for b in range(NB):
                o = ost[n % 2]
                for tb in range(4):
                    ps = self.ps[4 + tb]
                    for k in range(8):
                        P.mm(ps[0:cn, :], w_[:, k, 0:cn], hT[:, b, k, halo + tb * 512: halo + (tb + 1) * 512],
                             start=(k == 0), stop=(k == 7), inc=(k == 7))
                    P.copy("act" if tb % 2 == 0 else "dve", o[0:cn, tb * 512:(tb + 1) * 512], ps[0:cn, :])
                P.dma("pool", PT.v(("c", ci, b), (slice(c0, c0 + cn), slice(b * T, (b + 1) * T))), o[0:cn, :])
                n += 1

    def rwkv(self, PT0, YM, VTM, BON):
        P = self.P
        W = self.W
        C = 64
        NCH = T // C
        sb = P.sb
        def cols(name, src, off, n, p):
            t = sb(name, [p, n])
            self.col_load("sp", t, src, slice(off, off + n * p), n, p)
            return t
        mix_r = cols("rw_mixr", W["rwkv_mix"], 0, 8, 64)
        mix_k = cols("rw_mixk", W["rwkv_mix"], 512, 8, 64)
        mix_v = cols("rw_mixv", W["rwkv_mix"], 1024, 8, 64)
        mix_w = cols("rw_mixw", W["rwkv_mix"], 1536, 1, 64)
        mix_a = cols("rw_mixa", W["rwkv_mix"], 1600, 1, 64)
        mix_g = cols("rw_mixg", W["rwkv_mix"], 1664, 1, 128)
        w0 = cols("rw_w0", W["rwkv_w0"], 0, 8, 64)
        a0 = cols("rw_a0", W["rwkv_a0"], 0, 8, 64)
        k_k = cols("rw_kk", W["rwkv_k_k"], 0, 8, 64)
        k_a = cols("rw_ka", W["rwkv_k_a"], 0, 8, 64)
        omk = sb("rw_omk", [64, 8])
        P.ts("dve", omk, k_a, -1.0, 1.0, ALU.mult, ALU.add)
        rk = sb("rw_rk", [64, 8])
        with P.nc.allow_non_contiguous_dma(reason="small"):
            P.dma("sp", rk, V(W["rwkv_r_k"].v("c").buf, W["rwkv_r_k"].ap.rearrange("h j -> j h")))
        w2 = sb("rw_w2", [64, 512])
        P.dma("sp", w2, W["rwkv_w2"].v("c"))
        a2 = sb("rw_a2", [64, 512])
        P.dma("sp", a2, W["rwkv_a2"].v("c"))
        g2 = sb("rw_g2", [128, 512])
        P.dma("sp", g2, W["rwkv_g2"].v("c"))
        lng = sb("rw_lng", [128, 512])
        P.dma("sp", lng, V(W["rwkv_ln_g"].v("c").buf, W["rwkv_ln_g"].ap[0, :].partition_broadcast(128)))
        lnb = sb("rw_lnb", [128, 512])
        P.dma("sp", lnb, V(W["rwkv_ln_b"].v("c").buf, W["rwkv_ln_b"].ap[0, :].partition_broadcast(128)))
        o64 = sb("rw_o64", [64, 512])
        P.memset("pool", o64, 1.0)
        msu = sb("rw_msu", [64, 512])
        mu = sb("rw_mu", [64, 512])
        msl = sb("rw_msl", [64, 512])
        i8 = sb("rw_i8", [64, 512])
        pat = [[0, 8], [1, 64]]
        P.affsel(msu.re("p (a b) -> p a b", a=8), o64.re("p (a b) -> p a b", a=8), pat, ALU.is_gt, 0.0, 0, -1)
        P.affsel(mu.re("p (a b) -> p a b", a=8), o64.re("p (a b) -> p a b", a=8), pat, ALU.is_ge, 0.0, 0, -1)
        P.affsel(msl.re("p (a b) -> p a b", a=8), o64.re("p (a b) -> p a b", a=8), [[0, 8], [-1, 64]], ALU.is_gt, 0.0, 0, 1)
        P.affsel(i8.re("p (a b) -> p a b", a=8), o64.re("p (a b) -> p a b", a=8), pat, ALU.is_equal, 0.0, 0, -1)
        m01 = sb("rw_m01", [64, T], BF16)
        P.memset("pool", m01, 1.0)
        P.memset("pool", m01.re("p (c t) -> p c t", t=C)[:, :, 0:1], 0.0)
        X0 = sb("rw_X0", [64, T + 1]); X1 = sb("rw_X1", [64, T + 1]); X2 = sb("rw_X2", [64, T + 1])
        for x_ in (X0, X1, X2):
            P.memset("pool", x_[:, 0:1], 0.0)
        TMPG = sb("rw_TMPG", [128, T]); TMP = TMPG[0:64]; B3 = sb("rw_B3", [64, T]); B4 = sb("rw_B4", [64, T]); B5 = sb("rw_B5", [64, T])
        B6 = sb("rw_B6", [64, T]); B7 = sb("rw_B7", [64, T]); E = sb("rw_E", [64, T])
        LW = sb("rw_LW", [64, T + 1]); LA = sb("rw_LA", [64, T + 1]); LG = sb("rw_LG", [128, T + 1])
        for x_ in (LW, LA, LG):
            P.memset("pool", x_[:, 0:1], 0.0)
        GC = sb("rw_GC", [64, NCH]); CUMC = sb("rw_CUMC", [64, NCH]); BONS = sb("rw_BONS", [64, NCH])
        VM_all = sb("rw_VMall", [64, T]); Y1_all = E; MT_all = sb("rw_MTall", [64, T])
        D0_all = B7; RH_all = sb("rw_RHall", [64, T]); S_all = sb("rw_Sall", [64, T + 64])
        Y_all = TMP
        yt = [sb(f"rw_yt{i}", [128, 512]) for i in range(2)]
        vt = [sb(f"rw_vt{i}", [128, 512]) for i in range(2)]
        bo = [sb(f"rw_bo{i}", [128, 8]) for i in range(2)]
        st = [sb(f"rw_st{i}", [128, 8]) for i in range(2)]
        sq = [sb(f"rw_sq{i}", [128, 512]) for i in range(2)]
        nm = ["AM", "BME", "KME", "AAB", "ARB", "AAK", "ARK", "PA", "PTA", "PB", "PTB", "R", "AH", "XX1", "U0T"]
        bt = {n_: sb("rw_" + n_, [64, 512], BF16) for n_ in nm}
        opb = {n_: sb("rw_ob_" + n_, [64, T], BF16) for n_ in ("RT", "AT")}
        b3b = V(B3.buf, B3.ap.bitcast(BF16))
        b4b = V(B4.buf, B4.ap.bitcast(BF16))
        opb["BT"] = b3b[:, 0:T]; opb["KT"] = b3b[:, T:2 * T]
        opb["BTE"] = b4b[:, 0:T]; opb["KTE"] = b4b[:, T:2 * T]
        VMb = sb("rw_VMb", [64, 512], BF16)
        DG32 = sb("rw_DG32", [64, 512])
        i8b = sb("rw_i8b", [64, 512], BF16)
        P.copy("pool", i8b, i8)
        P.memset("pool", S_all[:, 0:64], 0.0)
        ps = self.ps

        def shift(raw, mixcol, tmp):
            p = raw.ap.shape[0]
            P.tt("dve", tmp, raw[:, 0:T], raw[:, 1:T + 1], ALU.subtract)
            P.stt(raw[:, 1:T + 1], tmp, mixcol, raw[:, 1:T + 1], ALU.mult, ALU.add)

        def c3(v_):
            return v_.re("p (a b) -> p a b", b=64)

        for b in range(NB):
            tsl = slice(b * T, (b + 1) * T)
            P.dma("sp", LW[:, 1:], PT0.v(("c", 12, b), (slice(1536, 1600), tsl)))
            P.dma("sp", LA[:, 1:], PT0.v(("c", 12, b), (slice(1600, 1664), tsl)))
            P.dma("sp", LG[:, 1:], PT0.v(("c", 13, b), (slice(1664, 1792), tsl)))
            shift(LW, mix_w[:, 0:1], TMP)
            shift(LA, mix_a[:, 0:1], TMP)
            shift(LG, mix_g[:, 0:1], TMPG)
            P.act(LW[:, 1:], LW[:, 1:], AF.Tanh)
            P.act(LG[:, 1:], LG[:, 1:], AF.Sigmoid)
            if self.stop_after == "rw_lora":
                return
            for h in range(8):
                hs = slice(h * 64, (h + 1) * 64)
                hc = slice(h, h + 1)
                P.dma("sp", X0[:, 1:], PT0.v(("c", h // 2, b), (slice(h * 64, h * 64 + 64), tsl)))
                P.dma("sp", X1[:, 1:], PT0.v(("c", 4 + h // 2, b), (slice(512 + h * 64, 512 + h * 64 + 64), tsl)))
                P.dma("sp", X2[:, 1:], PT0.v(("c", 8 + h // 2, b), (slice(1024 + h * 64, 1024 + h * 64 + 64), tsl)))
                shift(X0, mix_r[:, hc], TMP)
                shift(X1, mix_k[:, hc], TMP)
                shift(X2, mix_v[:, hc], TMP)
                R_ = X0[:, 1:]; K_ = X1[:, 1:]; V_ = X2[:, 1:]
                for tb in range(4):
                    bs = slice(tb * 512, (tb + 1) * 512)
                    P.mm(ps[0][0:64, :], w2[:, hs], LW[:, 1 + tb * 512: 1 + (tb + 1) * 512])
                    P.act(B3[:, bs], ps[0][0:64, :], AF.Sigmoid, bias=w0[:, hc])
                    P.mm(ps[1][0:64, :], a2[:, hs], LA[:, 1 + tb * 512: 1 + (tb + 1) * 512])
                    P.act(B4[:, bs], ps[1][0:64, :], AF.Sigmoid, bias=a0[:, hc])
                LD = B3; A_ = B4
                P.ts("pool", LD, LD, -math.exp(-0.5), 0.0, ALU.mult, ALU.add)
                KK = B5
                P.act(KK, K_, AF.Copy, scale=k_k[:, hc])
                P.act(TMP, KK, AF.Square)
                for tb in range(4):
                    bs = slice(tb * 512, (tb + 1) * 512)
                    P.mm(ps[2 + tb % 2][0:64, :], self.ones[0:64, 0:64], TMP[:, bs])
                    P.ts("dve", E[:, bs], ps[2 + tb % 2][0:64, :], 1e-24, None, ALU.max)
                P.act(E, E, AF.Ln)
                P.act(E, E, AF.Exp, scale=-0.5)
                P.tt("dve", KK, KK, E, ALU.mult)
                KM = B6
                P.act(TMP, A_, AF.Identity, bias=omk[:, hc], scale=k_a[:, hc])
                P.tt("dve", KM, TMP, K_, ALU.mult)
                P.tt("dve", TMP, R_, KM, ALU.mult)
                for c in range(NCH):
                    P.mm(ps[4][0:64, c:c + 1], TMP[:, c * C:(c + 1) * C], rk[:, hc], inc=(c == NCH - 1))
                P.copy("act", BONS, ps[4][0:64, 0:NCH])
                with P.nc.allow_non_contiguous_dma(reason="small"):
                    P.dma("pool", BON.v(("bh", b, h), (slice(b * T, (b + 1) * T), slice(h, h + 1))).re("(c t) o -> t (c o)", t=C), BONS)
                CUM = B7
                P.scan(CUM, m01, LD, 0.0, ALU.mult, ALU.add)
                P.copy("pool", CUMC, CUM.re("p (c t) -> p c t", t=C)[:, :, C - 1])
                P.act(E, CUM, AF.Exp)
                P.copy("pool", GC, E.re("p (c t) -> p c t", t=C)[:, :, C - 1])
                P.tt("dve", R_, R_, E, ALU.mult)
                P.copy("act", opb["RT"], R_)
                P.tt("dve", TMP, CUM, LD, ALU.subtract)
                P.act(E, TMP, AF.Exp)
                P.stt(opb["AT"], KK, -1.0, E, ALU.mult, ALU.mult)
                P.tt("dve", KK, KK, A_, ALU.mult)
                P.act(E, CUM, AF.Exp, scale=-1.0)
                P.tt("dve", opb["BT"], KK, E, ALU.mult)
                P.tt("dve", opb["KT"], KM, E, ALU.mult)
                P.tt("pool", c3(TMP), CUMC.re("p (c o) -> p c o", o=1).bc([64, NCH, C]), c3(CUM), ALU.subtract)
                P.act(E, TMP, AF.Exp)
                P.tt("dve", opb["BTE"], KK, E, ALU.mult)
                P.tt("dve", opb["KTE"], KM, E, ALU.mult)
                RT32, VT = X0[:, 1:], X2[:, 1:]
                RT, AT, KT, BT, BTE, KTE = (opb[n_] for n_ in ("RT", "AT", "KT", "BT", "BTE", "KTE"))
                if self.stop_after == "rw_prep":
                    return
                for cb in range(NCH // 8):
                    bs = slice(cb * 512, (cb + 1) * 512)

                    def cs(v_, c):
                        return v_[:, cb * 512 + c * 64: cb * 512 + (c + 1) * 64]

                    def bsl(v_, c):
                        return v_[:, c * 64:(c + 1) * 64]

                    def mm8(pst, lf, rf, second=None):
                        for c in range(8):
                            if second is None:
                                P.mm(bsl(pst, c)[0:64], lf(c), rf(c), inc=(c == 7))
                            else:
                                P.mm(bsl(pst, c)[0:64], lf(c), rf(c), start=True, stop=False, inc=False)
                                P.mm(bsl(pst, c)[0:64], second[0](c), second[1](c), start=False, stop=True, inc=(c == 7))
                    for i_, (src, dst) in enumerate(((AT, bt["AM"]), (BTE, bt["BME"]), (KTE, bt["KME"]), (VT, VM_all[:, bs]))):
                        for c in range(8):
                            if i_ < 3:
                                P.mm(bsl(ps[i_], c)[0:64], cs(src, c), self.identb[0:64, 0:64], inc=(c == 7))
                            else:
                                P.tr(bsl(ps[i_], c)[0:64], cs(src, c), self.ident[0:64, 0:64], inc=(c == 7))
                        P.copy("act" if i_ % 2 == 0 else "dve", dst, ps[i_][0:64, :])
                    P.copy("act", VMb, VM_all[:, bs])
                    VM = VMb
                    mm8(ps[4], lambda c: cs(BT, c), lambda c: cs(AT, c))
                    P.tt("dve", bt["AAB"], ps[4][0:64, :], msu, ALU.mult)
                    mm8(ps[5], lambda c: cs(BT, c), lambda c: cs(RT, c))
                    P.tt("dve", bt["ARB"], ps[5][0:64, :], mu, ALU.mult)
                    mm8(ps[6], lambda c: cs(KT, c), lambda c: cs(AT, c))
                    P.tt("dve", bt["AAK"], ps[6][0:64, :], msu, ALU.mult)
                    mm8(ps[7], lambda c: cs(KT, c), lambda c: cs(RT, c))
                    P.tt("dve", bt["ARK"], ps[7][0:64, :], mu, ALU.mult)
                    mm8(ps[0], lambda c: cs(AT, c), lambda c: cs(BT, c))
                    P.tt("dve", bt["PTA"], ps[0][0:64, :], msl, ALU.mult)
                    P.tt("pool", bt["R"], bt["AAB"], i8b, ALU.add)
                    Pc, PTc = bt["AAB"], bt["PTA"]
                    nxt = [(bt["PB"], bt["PTB"]), (bt["PA"], bt["PTA"])]
                    for lvl in range(5):
                        Pn, PTn = nxt[lvl % 2]
                        if lvl < 4:
                            mm8(ps[1], lambda c: bsl(PTc, c), lambda c: bsl(Pc, c))
                            P.copy("act", Pn, ps[1][0:64, :])
                        mm8(ps[2], lambda c: bsl(Pc, c), lambda c: bsl(PTc, c))
                        P.copy("act", PTn, ps[2][0:64, :])
                        mm8(ps[3], lambda c: bsl(PTn, c), lambda c: bsl(bt["R"], c))
                        P.tt("dve", bt["R"], bt["R"], ps[3][0:64, :], ALU.add)
                        Pc, PTc = Pn, PTn
                    Ti = bt["R"]
                    mm8(ps[4], lambda c: bsl(Ti, c), lambda c: bsl(bt["AM"], c))
                    P.copy("act", bt["AH"], ps[4][0:64, :])
                    mm8(ps[5], lambda c: bsl(bt["AAK"], c), lambda c: bsl(VM, c))
                    P.copy("act", bt["XX1"], ps[5][0:64, :])
                    mm8(ps[6], lambda c: bsl(Ti, c), lambda c: bsl(bt["XX1"], c))
                    P.copy("act", bt["U0T"], ps[6][0:64, :])
                    mm8(ps[7], lambda c: bsl(bt["AH"], c), lambda c: bsl(bt["ARB"], c))
                    P.tt("dve", RH_all[:, bs], ps[7][0:64, :], RT32[:, bs], ALU.add)
                    mm8(ps[0], lambda c: bsl(bt["ARB"], c), lambda c: bsl(bt["U0T"], c),
                        second=(lambda c: bsl(bt["ARK"], c), lambda c: bsl(VM, c)))
                    P.copy("act", Y1_all[:, bs], ps[0][0:64, :])
                    mm8(ps[1], lambda c: bsl(bt["AH"], c), lambda c: bsl(bt["BME"], c))
                    P.tt("pool", c3(DG32), c3(i8), GC[:, cb * 8:(cb + 1) * 8].re("p (c o) -> p c o", o=1).bc([64, 8, C]), ALU.mult)
                    P.tt("dve", MT_all[:, bs], ps[1][0:64, :], DG32, ALU.add)
                    mm8(ps[2], lambda c: bsl(bt["BME"], c), lambda c: bsl(bt["U0T"], c),
                        second=(lambda c: bsl(bt["KME"], c), lambda c: bsl(VM, c)))
                    P.copy("act", D0_all[:, bs], ps[2][0:64, :])
                if self.stop_after == "rw_pre":
                    return
                import os as _os
                for c in range(int(_os.environ.get("RW_NSEQ", NCH))):
                    pss = ps[c % 8]
                    P.mm(pss[0:64, 0:64], MT_all[:, c * 64:(c + 1) * 64], S_all[:, c * 64:(c + 1) * 64])
                    P.tt("dve", S_all[:, (c + 1) * 64:(c + 2) * 64], pss[0:64, 0:64], D0_all[:, c * 64:(c + 1) * 64], ALU.add)
                if self.stop_after == "rw_seq":
                    return
                for cb in range(NCH // 8):
                    bs = slice(cb * 512, (cb + 1) * 512)
                    pst = ps[5 + cb % 2]
                    for c in range(8):
                        cc = cb * 8 + c
                        P.mm(pst[0:64, c * 64:(c + 1) * 64], RH_all[:, cc * 64:(cc + 1) * 64], S_all[:, cc * 64:(cc + 1) * 64], inc=(c == 7))
                    P.tt("dve", Y_all[:, bs], pst[0:64, :], Y1_all[:, bs], ALU.add)
                P.dma("pool", YM.v(("rw", b, h), (tsl, hs)).re("(c t) i -> t c i", t=C), c3(Y_all))
                P.dma("pool", VTM.v(("rw", b, h), (tsl, hs)).re("(c t) i -> t c i", t=C), c3(VM_all))
                if self.stop_after == "rw_y":
                    return
            P.barrier()
            def h3(v_):
                return v_.re("p (h i) -> p h i", i=64)
            def hb(v_):
                return v_.re("p (h o) -> p h o", o=1).bc([128, 8, 64])
            for tt in range(T // 128):
                r0 = b * T + tt * 128
                rs = slice(r0, r0 + 128)
                y_ = yt[tt % 2]; v_ = vt[tt % 2]; b_ = bo[tt % 2]; s_ = st[tt % 2]; q_ = sq[tt % 2]
                for h in range(8):
                    pass
                P.dma("sp", y_, V(YM.v(("rw", b, 0)).buf, YM.ap[rs, 0:512]))
                P.dma("sp", v_, V(VTM.v(("rw", b, 0)).buf, VTM.ap[rs, 0:512]))
                P.dma("sp", b_, V(BON.v(("bh", b, 0)).buf, BON.ap[rs, :]))
                pg = ps[tt % 2]
                P.mm(pg, LG[:, 1 + tt * 128: 1 + (tt + 1) * 128], g2)
                P.reduce(s_, h3(y_))
                P.ts("dve", s_, s_, -1.0 / 64, None, ALU.mult)
                P.tt("dve", h3(y_), h3(y_), hb(s_), ALU.add)
                P.act(q_, y_, AF.Square)
                P.reduce(s_, h3(q_))
                P.ts("dve", s_, s_, 1.0 / 64, 64e-5, ALU.mult, ALU.add)
                P.act(s_, s_, AF.Sqrt)
                P.recip(s_, s_)
                P.tt("dve", h3(y_), h3(y_), hb(s_), ALU.mult)
                P.tt("dve", y_, y_, lng, ALU.mult)
                P.tt("dve", y_, y_, lnb, ALU.add)
                P.tt("dve", h3(v_), h3(v_), hb(b_), ALU.mult)
                P.tt("dve", y_, y_, v_, ALU.add)
                P.tt("dve", y_, y_, pg, ALU.mult)
                P.dma("pool", V(YM.v(("rwo", b, tt)).buf, YM.ap[rs, 0:512]), y_)

    def layer0_mixer(self):
        P = self.P
        self.PT0 = self.scratch("PT0", [3336, NT])
        self.YM = self.scratch("YM", [NT, D])
        self.VTM = self.scratch("VTM", [NT, 512])
        self.BON = self.scratch("BON", [NT, 8])
        with P.phase():
            hT = P.sb("hT", [128, NB, 8, T], BF16)
            self.load_hT(self.x, hT, 0)
            tiles = [(i * 128, 128) for i in range(26)] + [(3328, 8)]
            self.inproj(hT, 0, self.W["l0_w_in"], tiles, self.PT0)
        if self.stop_after == "inproj0":
            return
        import os as _os
        if not _os.environ.get("SKIP_RWKV"):
            with P.phase():
                self.rwkv(self.PT0, self.YM, self.VTM, self.BON)
        if self.stop_after == "rwkv":
            return
        self.YBTd = self.scratch("YBTd", [128, NB * 4 * T], BF16)
        with P.phase():
            YBT = P.sb("mb_YBT", [128, NB, 4, T], BF16)
            self.mamba(self.PT0, YBT)
            P.dma("pool", self.YBTd.v("all"), YBT.re("p b j t -> p (b j t)"))
        if self.stop_after == "mamba":
            return
        self.H1 = self.scratch("H1", [NT, D])
        self.H2 = self.scratch("H2", [NT, D])
        with P.phase():
            YBT = P.sb("op_YBT", [128, NB, 4, T], BF16)
            P.dma("sp", YBT.re("p b j t -> p (b j t)"), self.YBTd.v("all"))
            self.outproj_ln(self.YM, YBT, "l0_w_out", "l0_ln1_g", "l0_ln1_b", self.x, self.H1)
        if self.stop_after == "h1":
            return
        with P.phase():
            self.ffn(self.H1, self.H2, "l0_ln2", "ffn0")
        if self.stop_after == "h2":
            return


def _in_map(inputs, c):
    m = {}
    for n, shp in WSPEC.items():
        a = np.ascontiguousarray(inputs[n], dtype=np.float32)
        m[n] = a if a.ndim == 2 else a.reshape(1, -1)
    m["x"] = np.ascontiguousarray(inputs["x"][2 * c:2 * c + 2]).reshape(NT, D)
    m["positions"] = np.ascontiguousarray(inputs["positions"][2 * c:2 * c + 2]).astype(np.int32)
    invf = (1.0 / (np.float32(10000.0) ** (np.arange(0, 32, 2, dtype=np.float32) / np.float32(32)))).astype(np.float32)
    rc = np.zeros((32, 2), np.float32)
    rc[:, 0] = np.concatenate([invf, invf])
    rc[:16, 1] = -1.0
    rc[16:, 1] = 1.0
    m["rope_c"] = rc
    return m


def _mamba(self, PT0, YBT):
    P = self.P
    W = self.W
    sb = P.sb
    ps = self.ps
    L = 128
    OFF = 1792
    cw = []
    for i in range(4):
        t = sb(f"mb_cw{i}", [128, 8])
        with P.nc.allow_non_contiguous_dma(reason="small"):
            P.dma("sp", t, V(W["ssm_conv_w"].v("c").buf, W["ssm_conv_w"].ap[i, :].rearrange("(j p) -> p j", p=128)))
        cw.append(t)
    cb_ = sb("mb_cb", [128, 8])
    self.col_load("sp", cb_, W["ssm_conv_b"], slice(0, 1024), 8, 128)
    ng = sb("mb_ng", [128, 4])
    self.col_load("sp", ng, W["ssm_norm_g"], slice(0, 512), 4, 128)
    dtb = sb("mb_dtb", [8, 1])
    self.col_load("sp", dtb, W["ssm_dt_bias"], slice(0, 8), 1, 8)
    alog = sb("mb_alog", [8, 1])
    self.col_load("sp", alog, W["ssm_a_log"], slice(0, 8), 1, 8)
    P.act(alog, alog, AF.Exp)
    P.ts("dve", alog, alog, -1.0, None, ALU.mult)
    dsk = sb("mb_dsk", [128, 4])
    with P.nc.allow_non_contiguous_dma(reason="small"):
        dv = W["ssm_d"].ap[0, 0:8].rearrange("(j two) -> two j", two=2)
        P.dma("sp", dsk[0:64, :], V(W["ssm_d"].v("c").buf, dv[0].partition_broadcast(64)))
        P.dma("sp", dsk[64:128, :], V(W["ssm_d"].v("c").buf, dv[1].partition_broadcast(64)))
    tri = sb("mb_tri", [128, 128])
    P.affsel(tri, self.ones, [[1, 128]], ALU.is_ge, 0.0, 0, -1)
    ZT = sb("mb_ZT", [128, 4, T])
    XC = sb("mb_XC", [128, 8, T])
    RAW = [sb(f"mb_RAW{i}", [128, 3 + T]) for i in range(1)]
    for r_ in RAW:
        P.memset("pool", r_[:, 0:3], 0.0)
    DTr = sb("mb_DT", [8, T]); ADT = sb("mb_ADT", [8, T])
    BTb = sb("mb_BTb", [128, 2, T], BF16); CTb = sb("mb_CTb", [128, 2, T], BF16)
    S = sb("mb_S", [128, 8, 64]); STp = sb("mb_STp", [128, 8, 128], BF16)
    Xp = [sb(f"mb_Xp{i}", [128, 8, 128], BF16) for i in range(2)]
    for x_ in Xp:
        P.memset("pool", x_, 0.0)
    Xtm = sb("mb_Xtm", [128, 8, 64]); Xd = sb("mb_Xd", [128, 8, 64], BF16)
    Btm = sb("mb_Btm", [128, 256], BF16)
    DTA = sb("mb_DTA", [128, 16]); CS = sb("mb_CS", [128, 8]); NCS = sb("mb_NCS", [128, 8])
    CBT = sb("mb_CBT", [128, 256]); ECE = sb("mb_ECE", [128, 8]); DECE = sb("mb_DECE", [128, 8])
    ABC4 = [sb(f"mb_ABC{i}", [128, 128]) for i in range(4)]
    DIF4 = [sb(f"mb_DIF{i}", [128, 512]) for i in range(2)]
    MTf4 = [sb(f"mb_MTf{i}", [128, 512]) for i in range(2)]
    MTb4 = [sb(f"mb_MTb{i}", [128, 512], BF16) for i in range(2)]
    ECU = sb("mb_ECU", [128, 4, 128])
    YT = sb("mb_YT", [128, 128]); SQ = sb("mb_SQ", [128, 2, 512]); RS = sb("mb_RS", [128, 512])
    for b in range(NB):
        tsl = slice(b * T, (b + 1) * T)
        P.memset("pool", S, 0.0)
        P.memset("pool", STp, 0.0)
        for j in range(4):
            P.dma("sp", ZT[:, j, :], PT0.v(("c", 14 + j, b), (slice(OFF + j * 128, OFF + (j + 1) * 128), tsl)))
            P.act(ZT[:, j, :], ZT[:, j, :], AF.Silu)
        for j in range(8):
            r_ = RAW[0]
            c0 = OFF + 512 + j * 128
            P.dma("sp", r_[:, 3:], PT0.v(("c", 18 + j, b), (slice(c0, c0 + 128), tsl)))
            acc = XC[:, j, :]
            P.ts("dve", acc, r_[:, 0:T], cw[0][:, j:j + 1], cb_[:, j:j + 1], ALU.mult, ALU.add)
            for i in range(1, 4):
                P.stt(acc, r_[:, i:i + T], cw[i][:, j:j + 1], acc, ALU.mult, ALU.add)
            P.act(acc, acc, AF.Silu)
        P.dma("sp", DTr, PT0.v(("c", 26, b), (slice(3328, 3336), tsl)))
        P.act(DTr, DTr, AF.Exp, bias=dtb[:, 0:1])
        P.act(DTr, DTr, AF.Ln, bias=1.0)
        P.ts("dve", ADT, DTr, alog[:, 0:1], None, ALU.mult)
        for g in range(2):
            P.copy("pool", BTb[:, g, :], XC[:, 4 + g, :])
            P.copy("pool", CTb[:, g, :], XC[:, 6 + g, :])
        for cc in range(T // L):
            csl = slice(cc * L, (cc + 1) * L)
            xp = Xp[cc % 2]
            for j in range(4):
                P.tr(ps[0][:, j * 128:(j + 1) * 128], XC[:, j, csl], self.ident, inc=(j == 3))
            for g in range(2):
                P.tr(ps[1][:, g * 128:(g + 1) * 128], XC[:, 4 + g, csl], self.ident, inc=False)
            P.tr(ps[1][:, 256:264], DTr[:, csl], self.ident[0:8, 0:8], inc=False)
            P.tr(ps[1][:, 264:272], ADT[:, csl], self.ident[0:8, 0:8], inc=True)
            P.copy("act", DTA, ps[1][:, 256:272])
            P.copy("act", Btm, ps[1][:, 0:256])
            P.tt("dve", Xtm, ps[0].re("p (h i) -> p h i", i=64), DTA[:, 0:8].re("p (h o) -> p h o", o=1).bc([128, 8, 64]), ALU.mult)
            P.copy("pool", xp[:, 0:8:2, 0:64], Xtm[:, 0:8:2, :])
            P.copy("pool", xp[:, 1:8:2, 64:128], Xtm[:, 1:8:2, :])
            P.mm(ps[2][:, 0:8], tri, DTA[:, 8:16])
            P.copy("act", CS, ps[2][:, 0:8])
            P.ts("dve", NCS, CS, -1.0, None, ALU.mult)
            for g in range(2):
                P.mm(ps[3][:, g * 128:(g + 1) * 128], BTb[:, g, csl], CTb[:, g, csl])
            P.copy("act", CBT, ps[3][:, 0:256])
            pcs = [ps[4], ps[5]]
            for g in range(2):
                for hh in range(4):
                    h = 4 * g + hh
                    P.ts("pool", ABC4[hh], self.ones, DTA[:, 8 + h:9 + h], 0.0, ALU.mult, ALU.add)
                    P.mm(pcs[g][:, hh * 128:(hh + 1) * 128], ABC4[hh], tri, inc=(hh == 3))
            for g in range(2):
                pc3 = pcs[g].re("p (h l) -> p h l", l=128)
                for hh in range(4):
                    h = 4 * g + hh
                    P.ts("dve", DIF4[g][:, hh * 128:(hh + 1) * 128], pcs[g][:, hh * 128:(hh + 1) * 128], CS[:, h:h + 1], 0.0, ALU.subtract, ALU.min)
                P.act(DIF4[g], DIF4[g], AF.Exp)
                P.act(ECE[:, 4 * g:4 * g + 4], pc3[:, :, 127], AF.Exp)
                P.tt("dve", DECE[:, 4 * g:4 * g + 4], pc3[:, :, 127], CS[:, 4 * g:4 * g + 4], ALU.subtract)
                P.act(DECE[:, 4 * g:4 * g + 4], DECE[:, 4 * g:4 * g + 4], AF.Exp)
                P.act(ECU[0:64, 2 * g:2 * g + 2, :], pc3[0:64, 0:4:2, :], AF.Exp)
                P.act(ECU[64:128, 2 * g:2 * g + 2, :], pc3[64:128, 1:4:2, :], AF.Exp)
                d3 = DIF4[g].re("p (h l) -> p h l", l=128)
                P.tt("dve", MTf4[g].re("p (h l) -> p h l", l=128), CBT[:, g * 128:(g + 1) * 128].re("p (o l) -> p o l", o=1).bc([128, 4, 128]), d3, ALU.mult)
                P.tt("pool", MTb4[g].re("p (h l) -> p h l", l=128), MTf4[g].re("p (h l) -> p h l", l=128), tri.re("p (o l) -> p o l", o=1).bc([128, 4, 128]), ALU.mult)
            for h in range(8):
                g = h // 4
                hh = h % 4
                pr = h // 2
                P.mm(ps[6][:, pr * 128:(pr + 1) * 128], xp[:, h, :], MTb4[g][:, hh * 128:(hh + 1) * 128], start=(h % 2 == 0), stop=(h % 2 == 1), inc=(h % 2 == 1))
                P.mm(ps[7][:, pr * 128:(pr + 1) * 128], STp[:, h, :], CTb[:, g, csl], start=(h % 2 == 0), stop=(h % 2 == 1), inc=(h % 2 == 1))
            for pr in range(4):
                P.tt("dve", YT, ps[7][:, pr * 128:(pr + 1) * 128], ECU[:, pr, :], ALU.mult)
                P.tt("dve", YT, ps[6][:, pr * 128:(pr + 1) * 128], YT, ALU.add)
                P.stt(YT, XC[:, pr, csl], dsk[:, pr:pr + 1], YT, ALU.mult, ALU.add)
                P.tt("pool", ZT[:, pr, csl], YT, ZT[:, pr, csl], ALU.mult)
            P.tt("dve", Xd, Xtm, DECE.re("p (h o) -> p h o", o=1).bc([128, 8, 64]), ALU.mult)
            for g in range(2):
                P.mm(ps[2][:, g * 256:(g + 1) * 256], Btm[:, g * 128:(g + 1) * 128], Xd[:, 4 * g:4 * g + 4, :].re("p h i -> p (h i)"))
            P.tt("pool", S, S, ECE.re("p (h o) -> p h o", o=1).bc([128, 8, 64]), ALU.mult)
            P.tt("dve", S, S, ps[2].re("p (h i) -> p h i", i=64), ALU.add)
            P.copy("pool", STp[:, 0:8:2, 0:64], S[:, 0:8:2, :])
            P.copy("pool", STp[:, 1:8:2, 64:128], S[:, 1:8:2, :])
        for tb in range(4):
            bs = slice(tb * 512, (tb + 1) * 512)
            for g in range(2):
                for j in range(2):
                    P.tt("pool", SQ[:, j, :], ZT[:, 2 * g + j, bs], ZT[:, 2 * g + j, bs], ALU.mult)
                for j in range(2):
                    P.mm(ps[tb % 2], self.ones, SQ[:, j, :], start=(j == 0), stop=(j == 1), inc=(j == 1))
                P.ts("dve", RS, ps[tb % 2], 1.0 / 256, 1e-5, ALU.mult, ALU.add)
                P.act(RS, RS, AF.Sqrt)
                P.recip(RS, RS)
                for j in range(2):
                    P.stt(YBT[:, b, 2 * g + j, bs], ZT[:, 2 * g + j, bs], ng[:, 2 * g + j:2 * g + j + 1], RS, ALU.mult, ALU.mult)


K.mamba = _mamba


def _bcast_vec(self, name, wdt, n):
    t = self.P.sb(name, [128, n])
    self.P.dma("sp", t, V(wdt.v("c").buf, wdt.ap[0, 0:n].partition_broadcast(128)))
    return t


def _res_ln(self, pss, xin_tile, gam, bet, out_tile, tmp_stats):
    P = self.P
    ALPHA = 4 ** 0.25
    st6, ag = tmp_stats
    for hf in range(2):
        sl = slice(hf * 512, (hf + 1) * 512)
        P.stt(out_tile[:, sl], xin_tile[:, sl], ALPHA, pss[hf], ALU.mult, ALU.add)
        P.op("dve", lambda hf=hf, sl=sl: self.P.nc.vector.bn_stats(out=st6[:, hf, :].ap, in_=out_tile[:, sl].ap), r=(out_tile,), w=(st6,))
    P.op("dve", lambda: self.P.nc.vector.bn_aggr(out=ag.ap, in_=st6.re("p a b -> p (a b)").ap), r=(st6,), w=(ag,))
    P.ts("dve", ag[:, 1:2], ag[:, 1:2], 1e-5, None, ALU.add)
    P.act(ag[:, 1:2], ag[:, 1:2], AF.Sqrt)
    P.recip(ag[:, 1:2], ag[:, 1:2])
    P.ts("dve", out_tile, out_tile, ag[:, 0:1], ag[:, 1:2], ALU.subtract, ALU.mult)
    P.tt("dve", out_tile, out_tile, gam, ALU.mult)
    P.tt("dve", out_tile, out_tile, bet, ALU.add)


def _outproj_ln(self, YM, YBT, wname, gname, bname, Hin, Hout, ya_from_ym_cols=512):
    P = self.P
    sb = P.sb
    ps = self.ps
    nk_tm = 4 if YBT is not None else 8
    gam = self.bcast_vec("op_g", self.W[gname], D)
    bet = self.bcast_vec("op_b", self.W[bname], D)
    wst = sb("op_wst", [128, 8, 512])
    wb = sb("op_wb", [128, 8, D], BF16)
    for hf in range(2):
        self.load_w(self.W[wname], 0, 8, hf * 512, 512, wb[:, :, hf * 512:(hf + 1) * 512], wst)
    yat = [sb(f"op_yat{i}", [128, nk_tm * 128]) for i in range(2)]
    yaT = [sb(f"op_yaT{i}", [128, nk_tm, 128], BF16) for i in range(2)]
    xin = [sb(f"op_xin{i}", [128, D]) for i in range(2)]
    ot = [sb(f"op_ot{i}", [128, D]) for i in range(2)]
    st6 = [sb(f"op_st{i}", [128, 2, 6]) for i in range(2)]
    ag = [sb(f"op_ag{i}", [128, 2]) for i in range(2)]
    n = 0
    for b in range(NB):
        for tt in range(T // 128):
            r0 = b * T + tt * 128
            rs = slice(r0, r0 + 128)
            i2 = n % 2
            P.dma("sp", yat[i2], V(YM.v(("rwo", b, tt)).buf, YM.ap[rs, 0:nk_tm * 128]))
            P.dma("sp", xin[i2], Hin.v(("t", b, tt), (rs, slice(None))))
            for q in range(nk_tm // 4):
                pt = ps[(n * 2 + q) % 2]
                for k in range(4):
                    P.tr(pt[:, k * 128:(k + 1) * 128], yat[i2][:, (q * 4 + k) * 128:(q * 4 + k + 1) * 128], self.ident, inc=(k == 3))
                P.copy("act", yaT[i2][:, q * 4:q * 4 + 4, :], pt.re("p (a c) -> p a c", a=4))
            pss = [ps[2 + (n % 2) * 2], ps[3 + (n % 2) * 2]]
            for hf in range(2):
                for k in range(8):
                    if k < nk_tm:
                        lhs = yaT[i2][:, k, :]
                    else:
                        lhs = YBT[:, b, k - 4, tt * 128:(tt + 1) * 128]
                    P.mm(pss[hf], lhs, wb[:, k, hf * 512:(hf + 1) * 512], start=(k == 0), stop=(k == 7), inc=(k == 7))
            self.res_ln(pss, xin[i2], gam, bet, ot[i2], (st6[i2], ag[i2]))
            P.dma("pool", Hout.v(("t", b, tt), (rs, slice(None))), ot[i2])
            n += 1


def _ffn(self, Hin, Hout, lname, fname):
    P = self.P
    sb = P.sb
    ps = self.ps
    W = self.W
    DFF = 2816
    NJ = DFF // 128
    TB = 1024
    gam = self.bcast_vec("ff_g", W[lname + "_g"], D)
    bet = self.bcast_vec("ff_b", W[lname + "_b"], D)
    cw = []
    for i in range(3):
        t = sb(f"ff_cw{i}", [128, NJ])
        with P.nc.allow_non_contiguous_dma(reason="small"):
            P.dma("sp", t, V(W[fname + "_conv_w"].v("c").buf, W[fname + "_conv_w"].ap[i, :].rearrange("(j p) -> p j", p=128)))
        cw.append(t)
    cb_ = sb("ff_cb", [128, NJ])
    self.col_load("sp", cb_, W[fname + "_conv_b"], slice(0, DFF), NJ, 128)
    hT = sb("ff_hT", [128, 1, 8, 2 + T], BF16)
    wst = [sb(f"ff_wst{i}", [128, 8, 256]) for i in range(2)]
    wd = sb("ff_wd", [128, NJ, D], BF16)
    for j in range(NJ):
        src = W[fname + "_w_down"].v("w", (slice(j * 128, (j + 1) * 128), slice(None)))
        st_ = wst[j % 2].re("p a b -> p (a b)")[:, 0:D]
        P.dma("sp", st_, src)
        P.copy("act" if j % 2 else "dve", wd[:, j, :], st_)
    wgu = [sb(f"ff_wgu{i}", [128, 8, 256], BF16) for i in range(2)]
    G = [sb(f"ff_G{i}", [128, TB + 2]) for i in range(2)]
    ACC = [sb(f"ff_ACC{i}", [128, TB]) for i in range(2)]
    AT = sb("ff_AT", [128, NJ, TB], BF16)
    xin = [sb(f"ff_xin{i}", [128, D]) for i in range(2)]
    ot = [sb(f"ff_ot{i}", [128, D]) for i in range(2)]
    st6 = [sb(f"ff_st{i}", [128, 2, 6]) for i in range(2)]
    ag = [sb(f"ff_ag{i}", [128, 2]) for i in range(2)]
    n = 0
    wup = W[fname + "_w_up"]
    for b in range(NB):
        self.load_hT(Hin, hT, 2, bsel=[b])
        for half in range(T // TB):
            t0 = half * TB
            def fetch(jf):
                if2 = jf % 2
                for q, c0 in enumerate((jf * 128, DFF + jf * 128)):
                    src = wup.v("w", (slice(None), slice(c0, c0 + 128))).re("(k p) c -> p k c", p=128)
                    P.dma("sp", wst[if2][:, :, q * 128:(q + 1) * 128], src)
                P.copy("act", wgu[if2], wst[if2])
            fetch(0)
            for j in range(NJ):
                i2 = j % 2
                wg = wgu[i2]
                if j + 1 < NJ:
                    fetch(j + 1)
                g_ = G[i2]
                for blk, (o0, on) in enumerate(((0, 512), (512, 512), (1024, 2))):
                    pg = ps[4 + blk % 2] if blk < 2 else ps[6]
                    for k in range(8):
                        P.mm(pg[:, 0:on], wg[:, k, 0:128], hT[:, 0, k, t0 + o0: t0 + o0 + on], start=(k == 0), stop=(k == 7), inc=(k == 7))
                    P.copy("act", g_[:, o0:o0 + on], pg[:, 0:on])
                a_ = ACC[i2]
                P.ts("dve", a_, g_[:, 0:TB], cw[0][:, j:j + 1], cb_[:, j:j + 1], ALU.mult, ALU.add)
                P.stt(a_, g_[:, 1:TB + 1], cw[1][:, j:j + 1], a_, ALU.mult, ALU.add)
                P.stt(a_, g_[:, 2:TB + 2], cw[2][:, j:j + 1], a_, ALU.mult, ALU.add)
                P.act(a_, a_, AF.Silu)
                for blk in range(2):
                    pu = ps[(j % 2) * 2 + blk]
                    for k in range(8):
                        P.mm(pu, wg[:, k, 128:256], hT[:, 0, k, 2 + t0 + blk * 512: 2 + t0 + (blk + 1) * 512], start=(k == 0), stop=(k == 7), inc=(k == 7))
                    P.tt("dve", AT[:, j, blk * 512:(blk + 1) * 512], a_[:, blk * 512:(blk + 1) * 512], pu, ALU.mult)
            for t8 in range(TB // 128):
                tt = (t0 // 128) + t8
                r0 = b * T + tt * 128
                rs = slice(r0, r0 + 128)
                i2 = n % 2
                P.dma("sp", xin[i2], Hin.v(("t", b, tt), (rs, slice(None))))
                pss = [ps[(n % 2) * 2], ps[(n % 2) * 2 + 1]]
                for hf in range(2):
                    for j in range(NJ):
                        P.mm(pss[hf], AT[:, j, t8 * 128:(t8 + 1) * 128], wd[:, j, hf * 512:(hf + 1) * 512], start=(j == 0), stop=(j == NJ - 1), inc=(j == NJ - 1))
                self.res_ln(pss, xin[i2], gam, bet, ot[i2], (st6[i2], ag[i2]))
                P.dma("pool", Hout.v(("t", b, tt), (rs, slice(None))), ot[i2])
                n += 1


K.bcast_vec = _bcast_vec
K.res_ln = _res_ln
K.outproj_ln = _outproj_ln
K.ffn = _ffn


def _copy_out(self, src, dst):
    P = self.P
    tl = [P.sb(f"co_t{i}", [128, D]) for i in range(2)]
    n = 0
    for b in range(NB):
        for tt in range(T // 128):
            r0 = b * T + tt * 128
            rs = slice(r0, r0 + 128)
            P.dma("sp", tl[n % 2], src.v(("t", b, tt), (rs, slice(None))))
            P.dma("pool", dst.v(("t", b, tt), (rs, slice(None))), tl[n % 2])
            n += 1


K.copy_out = _copy_out


def build(stop_after=None, dbg=()):
    k = K(stop_after=stop_after, dbg=dbg)
    P = k.P
    k.layer0_mixer()
    if stop_after in ("inproj0", "rwkv", "mamba", "h1", "h2"):
        k.P.finish()
        return k
    k.layer1_mixer(k.H2)
    if stop_after in ("inproj1", "sb", "mla"):
        k.P.finish()
        return k
    k.H3 = k.scratch("H3", [NT, D])
    with P.phase():
        k.outproj_ln(k.YM, None, "l1_w_out", "l1_ln1_g", "l1_ln1_b", k.H2, k.H3)
    if stop_after == "h3":
        k.P.finish()
        return k
    with P.phase():
        k.ffn(k.H3, k.out, "l1_ln2", "ffn1")
    k.P.finish()
    return k


def kernel(**inputs):
    k = build()
    in_maps = [_in_map(inputs, c) for c in range(8)]
    res = run_bass_kernel_spmd(k.P.nc, in_maps, core_ids=list(range(8)))
    outs = [np.asarray(r["out"]).reshape(NB, T, D) for r in res.results]
    return np.concatenate(outs, axis=0).astype(np.float32)


def _l1_inproj(self, Hin, PT1, VSB):
    P = self.P
    with P.phase():
        hT = P.sb("hT1", [128, NB, 8, T], BF16)
        self.load_hT(Hin, hT, 0)
        tiles = [(i * 128, 128) for i in range(8)] + [(i * 128, 128) for i in range(12, 15)] + [(1920, 32)]
        self.inproj(hT, 0, self.W["l1_w_in"], tiles, PT1)
        wst = P.sb("ip1_wst", [128, 8, 32]); wb = P.sb("ip1_wb", [128, 8, 32], BF16)
        wdt = self.W["l1_w_in"]
        for q, c0 in enumerate((1936, 1920)):
            src = wdt.v("w", (slice(None), slice(c0, c0 + 16))).re("(k p) c -> p k c", p=128)
            with P.nc.allow_non_contiguous_dma(reason="small"):
                P.dma("sp", wst[:, :, q * 16:(q + 1) * 16], src)
        P.copy("pool", wb, wst)
        o = P.sb("ip1_o", [32, T])
        for b in range(NB):
            for tb in range(4):
                ps = self.ps[tb % 2]
                for k in range(8):
                    P.mm(ps[0:32, :], wb[:, k, :], hT[:, b, k, tb * 512:(tb + 1) * 512], start=(k == 0), stop=(k == 7), inc=(k == 7))
                P.copy("act", o[:, tb * 512:(tb + 1) * 512], ps[0:32, :])
            P.dma("pool", PT1.v(("sw", b), (slice(1952, 1984), slice(b * T, (b + 1) * T))), o)
        wst2 = P.sb("ip1_wst2", [128, 8, 512]); wv = P.sb("ip1_wv", [128, 8, 512], BF16)
        self.load_w(wdt, 0, 8, 1024, 512, wv, wst2)
        vt = [P.sb(f"ip1_vt{i}", [128, 512], BF16) for i in range(2)]
        n = 0
        for b in range(NB):
            for tt in range(T // 128):
                ps = self.ps[2 + n % 2]
                for k in range(8):
                    P.mm(ps, hT[:, b, k, tt * 128:(tt + 1) * 128], wv[:, k, :], start=(k == 0), stop=(k == 7), inc=(k == 7))
                P.copy("act" if n % 2 else "dve", vt[n % 2], ps)
                r0 = b * T + tt * 128
                P.dma("pool", VSB.v(("t", b, tt), (slice(r0, r0 + 128), slice(None))), vt[n % 2])
                n += 1


def _attn_consts(self, mode):
    P = self.P
    o5 = P.sb("at_o5", [128, 512])
    P.memset("pool", o5, 1.0)
    self.m_incl = []
    self.m_strict = []
    for j in range(4):
        if mode == "sm":
            mi = P.sb(f"at_mi{j}", [128, 512], BF16)
            P.affsel(mi, o5, [[1, 512]], ALU.is_ge, 0.0, -128 * j, -1)
            self.m_incl.append(mi)
        else:
            ms = P.sb(f"at_ms{j}", [128, 512])
            P.affsel(ms, o5, [[1, 512]], ALU.is_gt, 0.0, -128 * j, -1)
            self.m_strict.append(ms)
    if mode == "sb":
        self.tris = P.sb("at_tris", [128, 128])
        P.affsel(self.tris, self.ones, [[-1, 128]], ALU.is_gt, 0.0, 0, 1)


def _attn(self, mode, QT, KT, kd, Vt, vw, YM, b, col0, scale, nh=1):
    P = self.P
    ps = self.ps
    sb = P.sb
    tg = mode
    if getattr(self, "_at_ws_phase", None) is not P.phase_es:
        self._at_ws_phase = P.phase_es
        wsl = self._at_ws = []
        for u in range(nh):
            ws = {}
            ws["attT"] = sb(f"at_attT{tg}{u}", [128, 16, 512], BF16)
            ws["yo"] = [sb(f"at_yo{tg}{u}{i}", [128, 4, 64]) for i in range(2)]
            ws["rs"] = sb(f"at_rs{tg}{u}", [128, 4])
            if mode == "sb":
                ws["E1"] = [sb(f"at_E1{tg}{u}{i}", [128, 512]) for i in range(2)]
                ws["LK"] = [sb(f"at_LK{tg}{u}{i}", [128, 512]) for i in range(2)]
                ws["T1"] = [sb(f"at_T1{tg}{u}{i}", [128, 512]) for i in range(2)]
                ws["ACC"] = sb(f"at_ACC{tg}{u}", [128, 512])
            wsl.append(ws)
    wsl = self._at_ws
    cnt = 0
    for hg in range(8 // nh):
        for qb in range(4):
            qs_ = slice(qb * 512, (qb + 1) * 512)
            nkb = 4 * qb + 4
            kbs = list(range(nkb - 1, -1, -1))

            def s12(it, u):
                kb = kbs[it]
                j = kb - 4 * qb
                c0 = 128 * j if j > 0 else 0
                cs_ = slice(c0, 512)
                h = hg * nh + u
                ws = wsl[u]
                attT = ws["attT"]
                pz = ps[u] if nh > 1 else ps[it % 2]
                P.mm(pz[:, cs_], KT[0:kd, h, kb * 128:(kb + 1) * 128], QT[0:kd, h, qb * 512 + c0:(qb + 1) * 512])
                if mode == "sm":
                    P.act(attT[:, kb, cs_], pz[:, cs_], AF.Exp, scale=scale)
                    if j >= 0:
                        P.tt("pool", attT[:, kb, cs_], attT[:, kb, cs_], self.m_incl[j][:, cs_], ALU.mult)
                else:
                    i2 = it % 2
                    E1, LK, T1 = ws["E1"][i2], ws["LK"][i2], ws["T1"][i2]
                    P.act(E1[:, cs_], pz[:, cs_], AF.Exp, scale=scale)
                    P.act(LK[:, cs_], E1[:, cs_], AF.Ln, bias=1.0)
                    P.stt(T1[:, cs_], pz[:, cs_], scale, LK[:, cs_], ALU.mult, ALU.subtract)
                    if j >= 0:
                        P.tt("pool", LK[:, cs_], LK[:, cs_], self.m_strict[j][:, cs_], ALU.mult)

            def s34(it, u):
                kb = kbs[it]
                j = kb - 4 * qb
                c0 = 128 * j if j > 0 else 0
                cs_ = slice(c0, 512)
                ws = wsl[u]
                attT = ws["attT"]
                i2 = it % 2
                LK, T1, ACC = ws["LK"][i2], ws["T1"][i2], ws["ACC"]
                pc = ps[2 + u] if nh > 1 else ps[2 + it % 2]
                P.mm(pc[:, cs_], self.tris, LK[:, cs_], start=True, stop=(it == 0), inc=(it == 0))
                if it > 0:
                    P.mm(pc[:, cs_], self.ones, ACC[:, cs_], start=False, stop=True)
                P.tt("dve", T1[:, cs_], T1[:, cs_], pc[:, cs_], ALU.subtract)
                P.act(attT[:, kb, cs_], T1[:, cs_], AF.Exp)
                if j >= 0:
                    P.tt("pool", attT[:, kb, cs_], attT[:, kb, cs_], self.m_strict[j][:, cs_], ALU.mult)
                if it == 0:
                    if c0 > 0:
                        P.memset("pool", ACC[:, 0:c0], 0.0)
                    P.copy("pool", ACC[:, cs_], LK[:, cs_])
                elif kb > 0:
                    P.tt("dve", ACC[:, cs_], ACC[:, cs_], LK[:, cs_], ALU.add)

            for it in range(nkb + 1):
                if it < nkb:
                    for u in range(nh):
                        s12(it, u)
                if mode == "sb" and it >= 1:
                    for u in range(nh):
                        s34(it - 1, u)
            for u in range(nh):
                h = hg * nh + u
                ws = wsl[u]
                attT = ws["attT"]
                py = ps[4 + cnt % 2]
                y_ = ws["yo"][qb % 2]
                rs_ = ws["rs"]
                cnt += 1
                for qs in range(4):
                    last = 4 * qb + qs
                    for kb in range(last + 1):
                        P.mm(py[:, qs * vw:(qs + 1) * vw], attT[:, kb, qs * 128:(qs + 1) * 128], Vt[:, kb, h, :],
                             start=(kb == 0), stop=(kb == last), inc=(kb == last))
                pv = py[:, 0:4 * vw].re("p (a c) -> p a c", c=vw)
                if mode == "sm":
                    P.copy("act", rs_, pv[:, :, 64])
                    P.recip(rs_, rs_)
                    P.tt("dve", y_, pv[:, :, 0:64], rs_.re("p (a o) -> p a o", o=1).bc([128, 4, 64]), ALU.mult)
                else:
                    P.copy("act", y_, pv)
                r0 = b * T + qb * 512
                dst = V(YM.v(("at", mode, b, h, qb)).buf, YM.ap[r0:r0 + 512, col0 + h * 64: col0 + (h + 1) * 64].rearrange("(a p) c -> p a c", p=128))
                P.dma("pool", dst, y_)


K.l1_inproj = _l1_inproj
K.attn_consts = _attn_consts
K.attn = _attn


def _layer1_mixer(self, Hin):
    P = self.P
    W = self.W
    sb = P.sb
    ps = self.ps
    PT1 = self.PT1 = self.scratch("PT1", [1984, NT])
    VSB = self.VSB = self.scratch("VSB", [NT, 512], BF16)
    YM = self.YM
    self.l1_inproj(Hin, PT1, VSB)
    if self.stop_after == "inproj1":
        return
    with P.phase():
        self.attn_consts("sb")
        QT = sb("sb_QT", [64, 8, T], BF16); KT = sb("sb_KT", [64, 8, T], BF16)
        Vt = sb("sb_Vt", [128, 16, 8, 64], BF16)
        stg = [sb(f"sb_stg{i}", [64, T]) for i in range(2)]
        n = 0
        for b in range(NB):
            tsl = slice(b * T, (b + 1) * T)
            for h in range(8):
                for q, (dst, r0) in enumerate(((QT, h * 64), (KT, 512 + h * 64))):
                    s_ = stg[n % 2]
                    P.dma("sp", s_, PT1.v(("c", r0 // 128, b), (slice(r0, r0 + 64), tsl)))
                    P.copy("dve" if n % 2 else "act", dst[:, h, :], s_)
                    n += 1
            P.barrier()
            P.dma("sp", Vt.re("p a h c -> p a (h c)"), V(VSB.v(("t", b, 0)).buf, VSB.ap[tsl, :].rearrange("(a p) c -> p a c", p=128)))
            self.attn("sb", QT, KT, 64, Vt, 64, YM, b, 0, 64 ** -0.5, nh=2)
    if self.stop_after == "sb":
        return
    with P.phase():
        self.attn_consts("sm")
        rc = sb("ml_rc", [96, 2])
        P.dma("sp", rc[64:96, :], self.rope_c.v("c"))
        qg = sb("ml_qg", [128, 2]); self.col_load("sp", qg, W["mla_q_norm_g"], slice(0, 256), 2, 128)
        kvg = sb("ml_kvg", [128, 1]); self.col_load("sp", kvg, W["mla_kv_norm_g"], slice(0, 128), 1, 128)
        CQ = sb("ml_CQ", [128, 2, T]); CKV = sb("ml_CKV", [128, T])
        wst = CQ[:, :, 0:768]
        wuq = sb("ml_wuq", [128, 2, 768], BF16)
        P.dma("sp", wst, W["mla_w_uq"].v("w").re("(k p) c -> p k c", p=128))
        P.copy("pool", wuq, wst)
        wsw = sb("ml_wsw", [128, 2, 8, 96], BF16)
        P.memset("pool", wsw, 0.0)
        w4 = wst.re("p k (h c) -> p k h c", c=96)
        P.copy("pool", wsw[:, :, :, 64:80], w4[:, :, :, 80:96])
        P.copy("pool", wsw[:, :, :, 80:96], w4[:, :, :, 64:80])
        wst2 = CKV[:, 0:1024]
        P.dma("sp", wst2, W["mla_w_ukv"].v("w"))
        wuk = sb("ml_wuk", [128, 8, 64], BF16); wv = sb("ml_wv", [128, 8, 64], BF16)
        w5 = wst2.re("p (h c) -> p h c", c=128)
        P.copy("pool", wuk, w5[:, :, 0:64])
        P.copy("pool", wv, w5[:, :, 64:128])
        QT = sb("ml_QT", [96, 8, T], BF16); KT = sb("ml_KT", [96, 8, T], BF16)
        Vt = sb("ml_Vt", [128, 16, 8, 65], BF16)
        P.memset("pool", Vt, 1.0)
        CQN = sb("ml_CQN", [128, 2, T], BF16); CKVN = sb("ml_CKVN", [128, T], BF16)
        SQ = sb("ml_SQ", [128, 2, 512]); RS = sb("ml_RS", [128, 512])
        ANG = sb("ml_ANG", [96, T]); COS2 = sb("ml_COS", [96, T]); SINS = sb("ml_SIN", [96, T])
        KPE = sb("ml_KPE", [96, T]); KSW = sb("ml_KSW", [96, T])
        KQ = KSW
        POSI = V(KPE.buf, KPE.ap.bitcast(I32))
        TA = [sb(f"ml_TA{i}", [96, 512]) for i in range(2)]; TB_ = [sb(f"ml_TB{i}", [96, 512]) for i in range(2)]
        R = slice(64, 96)
        TWO_PI = 2.0 * math.pi
        C1 = 6.28125
        C2 = float(np.float32(TWO_PI - C1))
        C3 = float(TWO_PI - C1 - C2)
        MAGIC = 12582912.0
        for b in range(NB):
            tsl = slice(b * T, (b + 1) * T)
            for j in range(2):
                P.dma("sp", CQ[:, j, :], PT1.v(("c", 12 + j, b), (slice(1536 + j * 128, 1536 + (j + 1) * 128), tsl)))
            P.dma("sp", CKV, PT1.v(("c", 14, b), (slice(1792, 1920), tsl)))
            P.dma("sp", POSI[R, :], V(self.pos.v("c").buf, self.pos.ap[b, :].partition_broadcast(32)))
            for tb in range(4):
                bs = slice(tb * 512, (tb + 1) * 512)
                for j in range(2):
                    P.tt("pool", SQ[:, j, :], CQ[:, j, bs], CQ[:, j, bs], ALU.mult)
                for j in range(2):
                    P.mm(ps[tb % 2], self.ones, SQ[:, j, :], start=(j == 0), stop=(j == 1), inc=(j == 1))
                P.ts("dve", RS, ps[tb % 2], 1.0 / 256, 1e-6, ALU.mult, ALU.add)
                P.act(RS, RS, AF.Sqrt)
                P.recip(RS, RS)
                for j in range(2):
                    P.stt(CQN[:, j, bs], CQ[:, j, bs], qg[:, j:j + 1], RS, ALU.mult, ALU.mult)
                P.tt("pool", SQ[:, 0, :], CKV[:, bs], CKV[:, bs], ALU.mult)
                P.mm(ps[2 + tb % 2], self.ones, SQ[:, 0, :])
                P.ts("dve", RS, ps[2 + tb % 2], 1.0 / 128, 1e-6, ALU.mult, ALU.add)
                P.act(RS, RS, AF.Sqrt)
                P.recip(RS, RS)
                P.stt(CKVN[:, bs], CKV[:, bs], kvg[:, 0:1], RS, ALU.mult, ALU.mult)
            P.copy("dve", ANG[R, :], POSI[R, :])
            P.ts("dve", ANG[R, :], ANG[R, :], rc[R, 0:1], None, ALU.mult)
            P.ts("dve", KQ[R, :], ANG[R, :], 1.0 / TWO_PI, MAGIC, ALU.mult, ALU.add)
            P.ts("dve", KQ[R, :], KQ[R, :], MAGIC, None, ALU.subtract)
            P.stt(ANG[R, :], KQ[R, :], -C1, ANG[R, :], ALU.mult, ALU.add)
            P.stt(ANG[R, :], KQ[R, :], -C2, ANG[R, :], ALU.mult, ALU.add)
            P.stt(ANG[R, :], KQ[R, :], -C3, ANG[R, :], ALU.mult, ALU.add)
            P.ts("dve", ANG[R, :], ANG[R, :], math.pi, -math.pi, ALU.min, ALU.max)
            P.act(SINS[R, :], ANG[R, :], AF.Sin)
            P.ts("dve", SINS[R, :], SINS[R, :], rc[R, 1:2], None, ALU.mult)
            P.ts("dve", KQ[R, :], ANG[R, :], math.pi / 2, None, ALU.is_gt)
            P.stt(ANG[R, :], KQ[R, :], -TWO_PI, ANG[R, :], ALU.mult, ALU.add)
            P.ts("dve", ANG[R, :], ANG[R, :], math.pi / 2, math.pi, ALU.add, ALU.min)
            P.act(COS2[R, :], ANG[R, :], AF.Sin)
            P.dma("sp", KPE[R, :], PT1.v(("c", 15, b), (slice(1920, 1952), tsl)))
            P.dma("sp", KSW[R, :], PT1.v(("sw", b), (slice(1952, 1984), tsl)))
            P.tt("dve", KPE[R, :], KPE[R, :], COS2[R, :], ALU.mult)
            P.tt("dve", KSW[R, :], KSW[R, :], SINS[R, :], ALU.mult)
            P.tt("dve", KPE[R, :], KPE[R, :], KSW[R, :], ALU.add)
            for h in range(8):
                P.copy("pool" if h % 2 else "act", KT[R, h, :], KPE[R, :])
            n = 0
            for h in range(8):
                for tb in range(4):
                    bs = slice(tb * 512, (tb + 1) * 512)
                    pq = ps[n % 2]; pw = ps[2 + n % 2]; pk = ps[4 + n % 2]
                    for k in range(2):
                        P.mm(pq[0:96, :], wuq[:, k, h * 96:(h + 1) * 96], CQN[:, k, bs], start=(k == 0), stop=(k == 1), inc=(k == 1))
                    for k in range(2):
                        P.mm(pw[0:96, :], wsw[:, k, h, :], CQN[:, k, bs], start=(k == 0), stop=(k == 1), inc=(k == 1))
                    P.mm(pk[0:64, :], wuk[:, h, :], CKVN[:, bs])
                    P.copy("act", QT[0:64, h, bs], pq[0:64, :])
                    ta = TA[n % 2]; tb_ = TB_[n % 2]
                    P.tt("dve", ta[R, :], pq[R, :], COS2[R, bs], ALU.mult)
                    P.tt("dve", tb_[R, :], pw[R, :], SINS[R, bs], ALU.mult)
                    P.tt("pool", QT[R, h, bs], ta[R, :], tb_[R, :], ALU.add)
                    P.copy("act", KT[0:64, h, bs], pk[0:64, :])
                    n += 1
            for tt in range(T // 128):
                pv = ps[6 + tt % 2]
                P.mm(pv, CKVN[:, tt * 128:(tt + 1) * 128], wv.re("p h c -> p (h c)"))
                P.copy("act" if tt % 2 else "dve", Vt[:, tt, :, 0:64], pv.re("p (h c) -> p h c", c=64))
            self.attn("sm", QT, KT, 96, Vt, 65, YM, b, 512, 96 ** -0.5)


K.layer1_mixer = _layer1_mixer
```
